# Optimizing a Trainium2 kernel written in Bass

```python
import math
import jax, jax.numpy as jnp
from jax import lax
import numpy as np

D_MODEL = 1024
BATCH = 2
SEQ = 8192
DEPTH = 2

CTX_LEN = 256
GRID_W = 64
N_HEADS = 8
HEAD_DIM = 64
V_DIM = 2 * HEAD_DIM
QK_W = N_HEADS * 2 * HEAD_DIM
ATTN_WIDTH = N_HEADS * V_DIM
N_FGROUPS = 4
FGROUP_DIM = 128
FOURIER_WIDTH = N_FGROUPS * FGROUP_DIM
F_OFF = 2 * QK_W + ATTN_WIDTH
G_OFF = F_OFF + FOURIER_WIDTH
IN_WIDTH = G_OFF + 2 * D_MODEL
D_FF = 2816
AXIS_DIM = HEAD_DIM // 2
ROPE_BASE = 10000.0
Q_BLOCK = 128
N_MOD = 9
EPS = 1e-6

kernel_name = "hybrid_diffattn_fnet_macaron_dit"


def rms_norm(x, g):
    xf = x.astype(jnp.float32)
    y = xf * lax.rsqrt(jnp.mean(xf * xf, axis=-1, keepdims=True) + EPS)
    return (y * g.astype(jnp.float32)).astype(x.dtype)


def adaln_mod(cond, w_mod, b_mod):
    m = jax.nn.silu(cond) @ w_mod + b_mod
    return m.reshape(cond.shape[:-1] + (N_MOD, D_MODEL))


def modulate(x, m, i):
    shift = m[:, 3 * i][:, None]
    scale = m[:, 3 * i + 1][:, None]
    return x * (1 + scale) + shift


def swiglu(u, w_in, w_out):
    a, b = jnp.split(u @ w_in, 2, axis=-1)
    return (jax.nn.silu(a) * b) @ w_out


def ffn_sublayer(x, m, i, g, w_in, w_out):
    u = modulate(rms_norm(x, g), m, i)
    return x + 0.5 * m[:, 3 * i + 2][:, None] * swiglu(u, w_in, w_out)


def split_qkv(p):
    B, S = p.shape[:2]
    q = p[..., :QK_W].reshape(B, S, N_HEADS, 2, HEAD_DIM)
    k = p[..., QK_W:2 * QK_W].reshape(B, S, N_HEADS, 2, HEAD_DIM)
    v = p[..., 2 * QK_W:F_OFF].reshape(B, S, N_HEADS, V_DIM)
    return q, k, v


def rope_tables(S):
    ROWS = S // GRID_W
    row = jnp.broadcast_to(jnp.arange(ROWS, dtype=jnp.int32)[:, None], (ROWS, GRID_W)).reshape(S)
    col = jnp.broadcast_to(jnp.arange(GRID_W, dtype=jnp.int32)[None, :], (ROWS, GRID_W)).reshape(S)
    inv = 1.0 / (ROPE_BASE ** (jnp.arange(0, AXIS_DIM, 2, dtype=jnp.float32) / AXIS_DIM))

    def tab(pos):
        ang = pos.astype(jnp.float32)[:, None] * inv[None, :]
        ang = jnp.concatenate([ang, ang], axis=-1)
        return jnp.cos(ang), jnp.sin(ang)

    cr, sr = tab(row)
    cc, sc = tab(col)
    return cr, sr, cc, sc


def rotate(xh, cos, sin):
    cos = cos[None, :, None, None, :].astype(xh.dtype)
    sin = sin[None, :, None, None, :].astype(xh.dtype)
    x1, x2 = jnp.split(xh, 2, axis=-1)
    return xh * cos + jnp.concatenate([-x2, x1], axis=-1) * sin


def apply_rope2d(x, tabs):
    cr, sr, cc, sc = tabs
    return jnp.concatenate([rotate(x[..., :AXIS_DIM], cr, sr),
                            rotate(x[..., AXIS_DIM:], cc, sc)], axis=-1)


def diff_attend(q, k, v, lam):
    s = jnp.einsum("bqhcd,bkhcd->bhcqk", q, k).astype(jnp.float32) * (HEAD_DIM ** -0.5)
    p = jax.nn.softmax(s, axis=-1)
    a = p[:, :, 0] - lam * p[:, :, 1]
    return jnp.einsum("bhqk,bkhe->bqhe", a.astype(v.dtype), v)


def latent_diff_attention(q, k_all, v_all, lam):
    B, S = q.shape[:2]
    nb = S // Q_BLOCK
    qb = jnp.moveaxis(q.reshape(B, nb, Q_BLOCK, N_HEADS, 2, HEAD_DIM), 1, 0)
    out = lax.map(lambda blk: diff_attend(blk, k_all, v_all, lam), qb)
    return jnp.moveaxis(out, 0, 1).reshape(B, S, N_HEADS, V_DIM)


def fourier_mix(f):
    B, S = f.shape[:2]
    fg = f.reshape(B, S, N_FGROUPS, FGROUP_DIM).astype(jnp.float32)
    y = jnp.fft.fft2(fg, axes=(1, 3), norm="ortho").real
    return y.reshape(B, S, FOURIER_WIDTH).astype(f.dtype)


def mixer_merge(attn_heads, f, g_logits, lam_init, subln_g, w_attn_out, w_four_out, b_gate, w_o):
    B, S = attn_heads.shape[:2]
    a = (rms_norm(attn_heads, subln_g) * (1 - lam_init)).reshape(B, S, ATTN_WIDTH) @ w_attn_out
    fr = fourier_mix(f) @ w_four_out
    ga, gf = jnp.split(g_logits, 2, axis=-1)
    merged = jax.nn.sigmoid(ga + b_gate[0]) * a + jax.nn.sigmoid(gf + b_gate[1]) * fr
    return merged @ w_o


def setup_inputs(seed: int = 0) -> dict:
    key = jax.random.key(seed)
    ks = jax.random.split(key, 20)
    f32 = jnp.float32
    D = D_MODEL

    def nrm(k, shape, scale):
        return jax.random.normal(k, shape, f32) * scale

    return {
        "x": nrm(ks[0], (BATCH, SEQ, D), 1.0),
        "c": nrm(ks[1], (BATCH, D), 1.0),
        "ctx": nrm(ks[2], (BATCH, CTX_LEN, D), 1.0),
        "c_ctx": nrm(ks[3], (D,), 1.0),
        "w_mod": nrm(ks[4], (DEPTH, D, N_MOD * D), 0.5 * D ** -0.5),
        "b_mod": nrm(ks[5], (DEPTH, N_MOD * D), 0.01),
        "norm_g": 1.0 + nrm(ks[6], (DEPTH, 3, D), 0.1),
        "ffn_w_in": nrm(ks[7], (DEPTH, 2, D, 2 * D_FF), D ** -0.5),
        "ffn_w_out": nrm(ks[8], (DEPTH, 2, D_FF, D), D_FF ** -0.5),
        "w_in": nrm(ks[9], (DEPTH, D, IN_WIDTH), D ** -0.5),
        "b_gate": nrm(ks[10], (DEPTH, 2, D), 0.01),
        "lam_params": nrm(ks[11], (DEPTH, 4, HEAD_DIM), 0.1),
        "subln_g": 1.0 + nrm(ks[12], (DEPTH, V_DIM), 0.1),
        "w_attn_out": nrm(ks[13], (DEPTH, ATTN_WIDTH, D), ATTN_WIDTH ** -0.5),
        "w_four_out": nrm(ks[14], (DEPTH, FOURIER_WIDTH, D), FOURIER_WIDTH ** -0.5),
        "w_o": nrm(ks[15], (DEPTH, D, D), D ** -0.5),
        "final_g": 1.0 + nrm(ks[16], (D,), 0.1),
    }


def reference(x, c, ctx, c_ctx, w_mod, b_mod, norm_g, ffn_w_in, ffn_w_out, w_in, b_gate,
              lam_params, subln_g, w_attn_out, w_four_out, w_o, final_g):
    S = x.shape[1]
    tabs = rope_tables(S)
    x_lat = x
    x_ctx = ctx
    for l in range(DEPTH):
        last = l == DEPTH - 1
        lam_init = 0.8 - 0.6 * math.exp(-0.3 * l)
        lp = lam_params[l].astype(jnp.float32)
        lam = jnp.exp(jnp.sum(lp[0] * lp[1])) - jnp.exp(jnp.sum(lp[2] * lp[3])) + lam_init

        m_lat = adaln_mod(c, w_mod[l], b_mod[l])
        m_ctx = adaln_mod(c_ctx, w_mod[l], b_mod[l])[None]

        x_lat = ffn_sublayer(x_lat, m_lat, 0, norm_g[l, 0], ffn_w_in[l, 0], ffn_w_out[l, 0])
        x_ctx = ffn_sublayer(x_ctx, m_ctx, 0, norm_g[l, 0], ffn_w_in[l, 0], ffn_w_out[l, 0])

        u_lat = modulate(rms_norm(x_lat, norm_g[l, 1]), m_lat, 1)
        u_ctx = modulate(rms_norm(x_ctx, norm_g[l, 1]), m_ctx, 1)
        p_lat = u_lat @ w_in[l]
        q_l, k_l, v_l = split_qkv(p_lat)
        q_l = apply_rope2d(q_l, tabs)
        k_l = apply_rope2d(k_l, tabs)
        if last:
            kv_c = u_ctx @ w_in[l][:, QK_W:F_OFF]
            B = kv_c.shape[0]
            k_c = kv_c[..., :QK_W].reshape(B, CTX_LEN, N_HEADS, 2, HEAD_DIM)
            v_c = kv_c[..., QK_W:].reshape(B, CTX_LEN, N_HEADS, V_DIM)
        else:
            p_ctx = u_ctx @ w_in[l]
            q_c, k_c, v_c = split_qkv(p_ctx)

        k_all = jnp.concatenate([k_c, k_l], axis=1)
        v_all = jnp.concatenate([v_c, v_l], axis=1)
        attn_l = latent_diff_attention(q_l, k_all, v_all, lam)
        mix_l = mixer_merge(attn_l, p_lat[..., F_OFF:G_OFF], p_lat[..., G_OFF:], lam_init,
                            subln_g[l], w_attn_out[l], w_four_out[l], b_gate[l], w_o[l])
        x_lat = x_lat + m_lat[:, 5][:, None] * mix_l

        if not last:
            attn_c = diff_attend(q_c, k_c, v_c, lam)
            mix_c = mixer_merge(attn_c, p_ctx[..., F_OFF:G_OFF], p_ctx[..., G_OFF:], lam_init,
                                subln_g[l], w_attn_out[l], w_four_out[l], b_gate[l], w_o[l])
            x_ctx = x_ctx + m_ctx[:, 5][:, None] * mix_c
            x_ctx = ffn_sublayer(x_ctx, m_ctx, 2, norm_g[l, 2], ffn_w_in[l, 1], ffn_w_out[l, 1])

        x_lat = ffn_sublayer(x_lat, m_lat, 2, norm_g[l, 2], ffn_w_in[l, 1], ffn_w_out[l, 1])

    return rms_norm(x_lat, final_g)
```

```python
import math
import numpy as np
import ml_dtypes
import concourse.bass as bass
import concourse.mybir as mybir
from concourse.bass_utils import run_bass_kernel_spmd

F32 = mybir.dt.float32
BF16 = mybir.dt.bfloat16
AF = mybir.ActivationFunctionType
ALU = mybir.AluOpType
AX = mybir.AxisListType

D = 1024
DC = 8
DEPTH = 2
SEQ = 8192
CTX = 256
TL = 2048
TCX = 64
T = 2176
NKT = 17
TGS = [(0, 512), (512, 512), (1024, 512), (1536, 512), (2048, 128)]
DFF = 2816
NJ = 22
HBLK = [4, 4, 4, 4, 4, 2]
EPS = 1e-6
NH = 8
VW = 130
INW = 5632
SB_BASE = 16512
SB_END = 227328

COMPUTE = ("pe", "act", "dve", "pool")
SAME_ENGINE_SYNC = True
import os
FOU_DBG = int(os.environ.get("FOU_DBG", "9"))
FUSED = True


class Op:
    __slots__ = ("eng", "fn", "deps", "is_dma", "sem", "semval", "flag", "incval", "slot")

    def __init__(self, eng, fn, is_dma=False):
        self.eng = eng
        self.fn = fn
        self.deps = []
        self.is_dma = is_dma
        self.sem = None
        self.semval = 0
        self.flag = False
        self.incval = 0


class Prog:
    def __init__(self, nc):
        self.nc = nc
        self.ops = {e: [] for e in ("pe", "act", "dve", "pool", "sp")}
        self.lastw = {}
        self.readers = {}
        self.dma_sems = {}
        self.ctx = []

    def enter(self, cm):
        v = cm.__enter__()
        self.ctx.append(cm)
        return v

    def new_sem(self, name):
        return self.enter(self.nc.semaphore(name))

    def _track(self, op, reads, writes):
        deps = op.deps
        for k in reads:
            w = self.lastw.get(k)
            if w is not None:
                deps.append(w)
        for k in writes:
            w = self.lastw.get(k)
            if w is not None:
                deps.append(w)
            rs = self.readers.get(k)
            if rs:
                deps.extend(rs)
        for k in reads:
            self.readers.setdefault(k, []).append(op)
        for k in writes:
            self.lastw[k] = op
            self.readers[k] = []

    def op(self, eng, fn, reads=(), writes=()):
        o = Op(eng, fn)
        self._track(o, list(reads) + ["PHASE"], writes)
        self.ops[eng].append(o)
        return o

    def dma(self, eng, fn, slot, reads=(), writes=(), inc=16):
        o = Op(eng, fn, is_dma=True)
        self._track(o, list(reads) + ["PHASE"], writes)
        o.slot = slot
        o.deps = [d for d in o.deps if not (d.is_dma and d.slot == slot and d.eng == eng)]
        ent = self.dma_sems.get(slot)
        if ent is None:
            ent = [self.new_sem("d_%d" % len(self.dma_sems)), 0]
            self.dma_sems[slot] = ent
        ent[1] += inc
        o.sem = ent[0]
        o.semval = ent[1]
        o.incval = inc
        self.ops[eng].append(o)
        return o

    def barrier(self, fn):
        o = Op("pool", fn)
        self._track(o, [], ["PHASE"])
        self.ops["pool"].append(o)
        return o

    def finalize(self):
        for e, lst in self.ops.items():
            for o in lst:
                for d in o.deps:
                    if not d.is_dma:
                        if d.eng == o.eng and (d.eng == "pe" or not SAME_ENGINE_SYNC):
                            continue
                        d.flag = True
        self.esem = {}
        for e in COMPUTE:
            cnt = 0
            for o in self.ops[e]:
                if o.is_dma:
                    continue
                if o.flag:
                    cnt += 1
                    o.incval = cnt
            if cnt:
                self.esem[e] = self.new_sem("p_" + e)

    def emit_engine(self, e, engobj):
        waited = {}
        for o in self.ops[e]:
            need = {}
            for d in o.deps:
                if d.is_dma:
                    s, v = d.sem, d.semval
                else:
                    if d.eng == e and (e == "pe" or not SAME_ENGINE_SYNC):
                        continue
                    s, v = self.esem[d.eng], d.incval
                key = id(s)
                if v > need.get(key, (None, 0))[1]:
                    need[key] = (s, v)
            for key, (s, v) in need.items():
                if waited.get(key, 0) >= v:
                    continue
                engobj.wait_ge(s, v)
                waited[key] = v
            ins = o.fn()
            if o.is_dma:
                ins.then_inc(o.sem, o.incval)
            elif o.flag:
                ins.then_inc(self.esem[e], 1)
        sems = {}
        for o in self.ops[e]:
            if o.is_dma:
                sems[id(o.sem)] = (o.sem, max(o.semval, sems.get(id(o.sem), (None, 0))[1]))
        for key, (s, v) in sems.items():
            if waited.get(key, 0) < v:
                engobj.wait_ge(s, v)

    def run_blocks(self):
        nc = self.nc
        self.finalize()
        with nc.Block() as block:
            @block.tensor
            def _(eng):
                self.emit_engine("pe", eng)

            @block.scalar
            def _(eng):
                self.emit_engine("act", eng)

            @block.vector
            def _(eng):
                self.emit_engine("dve", eng)

            @block.gpsimd
            def _(eng):
                self.emit_engine("pool", eng)

            @block.sync
            def _(eng):
                self.emit_engine("sp", eng)

    def close(self):
        for cm in reversed(self.ctx):
            cm.__exit__(None, None, None)
        self.ctx = []


class Arena:
    def __init__(self, nc, start, end):
        self.nc, self.start, self.end, self.cur = nc, start, end, start
        self.n = 0

    def alloc(self, name, shape, dtype):
        esz = 4 if dtype == F32 else 2
        nbytes = int(np.prod(shape[1:])) * esz
        off = (self.cur + 63) // 64 * 64
        assert off + nbytes <= self.end, (name, off, nbytes, self.end)
        self.cur = off + nbytes
        self.n += 1
        return self.nc.alloc_sbuf_tensor_at(name, list(shape), dtype, offset=off)

    def sub(self):
        return Arena(self.nc, self.cur, self.end)


class Builder:
    def __init__(self, stages, fused, debug=False):
        self.stages = stages
        self.fused = fused
        self.debug = debug
        self.nc = nc = bass.Bass("TRN2", target_bir_lowering=False)
        self.P = Prog(nc)
        self.uid = 0
        self.ins = {}
        self.outs = {}

    def din(self, name, shape, dtype=F32):
        t = self.nc.dram_tensor(name, list(shape), dtype, kind="ExternalInput").ap()
        self.ins[name] = t
        return t

    def dout(self, name, shape, dtype=F32):
        t = self.nc.dram_tensor(name, list(shape), dtype, kind="ExternalOutput").ap()
        self.outs[name] = t
        return t

    def dint(self, name, shape, dtype=F32):
        return self.nc.dram_tensor(name, list(shape), dtype, kind="Internal").ap()

    def key(self, s):
        self.uid += 1
        return "%s#%d" % (s, self.uid)

    def mm(self, out, lhsT, rhs, start, stop, reads, writes, **kw):
        nc = self.nc
        self.P.op("pe", lambda: nc.tensor.matmul(out, lhsT=lhsT, rhs=rhs, start=start, stop=stop, **kw), reads, writes)

    def act(self, out, in_, func, reads, writes, **kw):
        nc = self.nc
        self.P.op("act", lambda: nc.scalar.activation(out=out, in_=in_, func=func, **kw), reads, writes)

    def tt(self, eng, out, in0, in1, op, reads, writes):
        nc = self.nc
        e = nc.vector if eng == "dve" else nc.gpsimd
        self.P.op(eng, lambda: e.tensor_tensor(out=out, in0=in0, in1=in1, op=op), reads, writes)

    def ts(self, eng, out, in0, s1, s2, op0, op1, reads, writes):
        nc = self.nc
        e = nc.vector if eng == "dve" else nc.gpsimd
        if op1 is None:
            self.P.op(eng, lambda: e.tensor_scalar(out=out, in0=in0, scalar1=s1, scalar2=None, op0=op0), reads, writes)
        else:
            self.P.op(eng, lambda: e.tensor_scalar(out=out, in0=in0, scalar1=s1, scalar2=s2, op0=op0, op1=op1), reads, writes)

    def stt(self, eng, out, in0, scalar, in1, op0, op1, reads, writes):
        nc = self.nc
        e = nc.vector if eng == "dve" else nc.gpsimd
        self.P.op(eng, lambda: e.scalar_tensor_tensor(out=out, in0=in0, scalar=scalar, in1=in1, op0=op0, op1=op1), reads, writes)

    def copy(self, eng, out, in_, reads, writes):
        nc = self.nc
        if eng == "act":
            self.P.op("act", lambda: nc.scalar.copy(out=out, in_=in_), reads, writes)
        else:
            e = nc.vector if eng == "dve" else nc.gpsimd
            self.P.op(eng, lambda: e.tensor_copy(out=out, in_=in_), reads, writes)

    def load(self, out, in_, slot, reads, writes, q="sp"):
        nc = self.nc
        if q == "sp":
            self.P.dma("sp", lambda: nc.sync.dma_start(out=out, in_=in_), slot, reads, writes)
        else:
            self.P.dma("pool", lambda: nc.gpsimd.dma_start(out=out, in_=in_), slot, reads, writes)

    def build(self):
        nc, P = self.nc, self.P
        st = self.stages
        first_l = int(st[0][1])
        self.xT_in = self.din("xT", [128, DC, T])
        self.cT = self.din("cT", [128, DC, 2])
        self.cossin = self.din("cossin", [128, 2, T])
        self.consts = self.din("consts", [128, 4, 128])
        self.dftch = self.din("dftch", [128, 2, 128], BF16)
        self.dftL = self.din("dftL", [8, 128, 64, 2, 256], BF16)
        self.dftC = self.din("dftC", [128, 4, 2, 256], BF16)
        layers = sorted(set(int(s[1]) for s in st))
        self.W = {}
        for l in layers:
            w = {}
            w["w_mod"] = self.din("w_mod%d" % l, [D, 9 * D])
            w["b_mod"] = self.din("b_mod%d" % l, [128, 72, 2])
            w["norm_g"] = self.din("norm_g%d" % l, [128, 3, DC])
            w["fw_in"] = self.din("fw_in%d" % l, [2, D, NJ * 256])
            w["fw_out"] = self.din("fw_out%d" % l, [2, DFF, D])
            w["w_in"] = self.din("w_in%d" % l, [D, INW])
            w["b_gate"] = self.din("b_gate%d" % l, [128, 16])
            w["lam"] = self.din("lam%d" % l, [128, 256])
            w["subln"] = self.din("subln%d" % l, [128, 128])
            w["w_ao"] = self.din("w_ao%d" % l, [D, D])
            w["w_fo"] = self.din("w_fo%d" % l, [512, D])
            w["w_o"] = self.din("w_o%d" % l, [D, D])
            self.W[l] = w
        self.final_g = self.din("final_g", [128, DC])

        self.X = {}
        for l in layers:
            e = {}
            hasA = ("A%d" % l) in st
            hasB = ("B%d" % l) in st
            def xt(kind, key, name, rows, cols):
                t32 = kind(name, [rows, cols // 2], F32)
                e.setdefault(key + "32", []).append(t32)
                e.setdefault(key, []).append(t32.bitcast(BF16))
            kd = self.dint if self.fused else self.dout
            kg = self.dint if self.fused else self.din
            if self.fused or hasA:
                for h in range(NH):
                    xt(kd, "kT_d", "kT_d%d_%d" % (l, h), 128, T)
                    xt(kd, "v_d", "v_d%d_%d" % (l, h), T, VW)
                for g in range(4):
                    xt(kd, "f_d", "f_d%d_%d" % (l, g), T, 128)
            if self.fused or hasB:
                for h in range(NH):
                    xt(kg, "kT_g", "kT_g%d_%d" % (l, h), 4 * 128, T)
                    xt(kg, "v_g", "v_g%d_%d" % (l, h), 4 * T, VW)
                for g in range(4):
                    xt(kg, "f_g", "f_g%d_%d" % (l, g), 4 * T, 128)
            if self.fused or (hasA and hasB):
                e["qT_d"] = self.dint("qT_d%d" % l, [NH * 128, T], BF16)
                e["sg_d"] = self.dint("sg_d%d" % l, [16 * 128, T], BF16)
            elif hasA:
                e["qT_d"] = self.dout("qT_d%d" % l, [NH * 128, T], BF16)
                e["sg_d"] = self.dout("sg_d%d" % l, [16 * 128, T], BF16)
            else:
                e["qT_d"] = self.din("qT_d%d" % l, [NH * 128, T], BF16)
                e["sg_d"] = self.din("sg_d%d" % l, [16 * 128, T], BF16)
            e["aT_d"] = (self.dout if self.debug else self.dint)("aT_d%d" % l, [NH * 128, T], BF16)
            self.X[l] = e
        last_stage = st[-1]
        if last_stage == "B1":
            self.out_ap = self.dout("outT", [128, DC, TL])
        else:
            self.out_ap = self.dout("xT_out", [128, DC, T])

        A = Arena(nc, SB_BASE, SB_END)
        self.XT = A.alloc("XT", [128, DC, T], F32)
        self.cst = A.alloc("cst", [128, 4, 128], F32)
        self.ones = self.cst[:, 0, :]
        self.RT = self.cst[:, 1, :]
        self.ident = self.cst[:, 2, :]
        self.epsb = self.cst[:, 3, :]
        self.bar = A.alloc("bar", [128, 8], F32)
        self.dch = A.alloc("dch", [128, 2, 128], BF16)
        self.mT = A.alloc("mT", [128, 72, 2], F32)
        self.bm = A.alloc("bm", [128, 72, 2], F32)
        self.ng = A.alloc("ng", [128, 3, DC], F32)
        self.gsc = A.alloc("gsc", [128, 3, DC, 2], F32)
        self.gth = A.alloc("gth", [128, 3, DC, 2], F32)
        self.silc = A.alloc("silc", [128, DC, 2], F32)
        self.bg = A.alloc("bg", [128, 16], F32)
        self.lamt = A.alloc("lamt", [128, 256], F32)
        self.lamv = A.alloc("lamv", [128, 8], F32)
        self.sgb = A.alloc("sgb", [128, 128], F32)
        self.fg = A.alloc("fg", [128, DC], F32)
        self.A0 = A.sub()
        self.ps = nc.alloc_psum_tensor("ps", [128, 4096], F32)

        self.load(self.XT[:], self.xT_in, "ld_x", [], ["XT"])
        self.load(self.cst[:], self.consts, "ld_c", [], ["cst"])
        self.load(self.dch[:], self.dftch, "ld_c2", [], ["dch"])
        self.load(self.fg[:], self.final_g, "ld_c3", [], ["fg"])

        self.barrier()
        for s in st:
            l = int(s[1])
            if s[0] == "A":
                self.phase_mod(l)
                self.barrier()
                self.ffn(l, 0)
                self.barrier()
                self.inproj(l)
                self.barrier()
                if self.fused:
                    self.exchange(l)
                    self.barrier()
            else:
                if first_l == l and st[0] == s:
                    self.phase_mod(l)
                    self.barrier()
                only = getattr(self, "only", ("att", "fou", "mrg", "ffn"))
                if "att" in only:
                    self.attention(l)
                    self.barrier()
                if "fou" in only:
                    self.fourier(l)
                    self.barrier()
                    if self.debug:
                        yd = self.dout("YT_dbg", [128, 4, T], BF16)
                        self.P.dma("sp", lambda: nc.sync.dma_start(out=yd, in_=self.YT[:]), "st_y", ["YT"], ["ydbg"])
                if "mrg" in only:
                    self.merge(l)
                    self.barrier()
                if "ffn" in only:
                    self.ffn(l, 1)
                    self.barrier()
        if last_stage == "B1":
            self.final_norm()
        else:
            nc_ = nc
            self.P.dma("sp", lambda: nc_.sync.dma_start(out=self.out_ap, in_=self.XT[:]), "st_x", ["XT"], ["out"])
        P.run_blocks()
        P.close()
        return nc

    def barrier(self):
        nc = self.nc
        self.P.barrier(lambda: nc.gpsimd.memset(self.bar[:, 0:1], 0.0))

    def bank(self, b, n=512, off=0):
        return self.ps[:, b * 512 + off: b * 512 + off + n]

    def phase_mod(self, l):
        nc, P = self.nc, self.P
        W = self.W[l]
        A = self.A0.sub()
        L = "m"
        wst = [A.alloc("wm%d_%d" % (l, i), [128, DC, 512], F32) for i in range(2)]
        cin = A.alloc("cin%d" % l, [128, DC, 2], F32)
        self.load(cin[:], self.cT, L + "c", [], [L + "cin"])
        self.load(self.bm[:], W["b_mod"], L + "b", ["bm"], ["bm"])
        self.load(self.ng[:], W["norm_g"], L + "g", ["ng"], ["ng"])
        self.load(self.bg[:], W["b_gate"], L + "bg", ["bg"], ["bg"])
        self.load(self.lamt[:], W["lam"], L + "lam", ["lamt"], ["lamt"])
        self.load(self.sgb[:], W["subln"], L + "sg", ["sgb"], ["sgb"])
        self.act(self.silc[:], cin[:], AF.Silu, [L + "cin"], ["silc"])
        wv = W["w_mod"].rearrange("(c p) n -> p c n", p=128)
        NS = 18
        for s in range(NS):
            buf = wst[s % 2]
            bk = "wm%d" % (s % 2)
            self.load(buf[:], wv[:, :, s * 512:(s + 1) * 512], (L, "w", s % 2), [], [bk])
            for jj in range(4):
                j = s * 4 + jj
                for c in range(DC):
                    self.mm(self.ps[:, j * 2:(j + 1) * 2], buf[:, c, jj * 128:(jj + 1) * 128], self.silc[:, c, :],
                            c == 0, c == DC - 1, [bk, "silc"], [("ps", 0)], skip_group_check=True)
        self.tt("dve", self.mT[:], self.ps[:, 0:144].rearrange("p (j s) -> p j s", s=2), self.bm[:], ALU.add,
                [("ps", 0), "bm"], ["mT"])
        for i in range(3):
            for s in range(2):
                sc = self.mT[:, (3 * i + 1) * 8:(3 * i + 2) * 8, s]
                gt = self.mT[:, (3 * i + 2) * 8:(3 * i + 3) * 8, s]
                self.stt("dve", self.gsc[:, i, :, s], sc, 1.0, self.ng[:, i, :], ALU.add, ALU.mult,
                         ["mT", "ng"], ["gsc"])
                self.ts("dve", self.gth[:, i, :, s], gt, 0.5 if i != 1 else 1.0, None, ALU.mult, None, ["mT"], ["gth"])
        lam_init = 0.8 - 0.6 * math.exp(-0.3 * l)
        tmp = A.alloc("lamtmp%d" % l, [128, 128], F32)
        self.tt("dve", tmp[:, 0:64], self.lamt[:, 0:64], self.lamt[:, 64:128], ALU.mult, ["lamt"], [L + "lt"])
        self.tt("dve", tmp[:, 64:128], self.lamt[:, 128:192], self.lamt[:, 192:256], ALU.mult, ["lamt"], [L + "lt"])
        P.op("dve", lambda: nc.vector.reduce_sum(out=self.lamv[:, 0:1], in_=tmp[:, 0:64], axis=AX.X), [L + "lt"], ["lamv"])
        P.op("dve", lambda: nc.vector.reduce_sum(out=self.lamv[:, 1:2], in_=tmp[:, 64:128], axis=AX.X), [L + "lt"], ["lamv"])
        self.act(self.lamv[:, 2:4], self.lamv[:, 0:2], AF.Exp, ["lamv"], ["lamv"])
        self.tt("dve", self.lamv[:, 4:5], self.lamv[:, 3:4], self.lamv[:, 2:3], ALU.subtract, ["lamv"], ["lamv"])
        self.ts("dve", self.lamv[:, 5:6], self.lamv[:, 4:5], -lam_init, None, ALU.add, None, ["lamv"], ["lamv"])
        self.ts("dve", self.sgb[:], self.sgb[:], 1.0 - lam_init, None, ALU.mult, None, ["sgb"], ["sgb"])

    def norm_mod(self, l, i, uT, ukey, A):
        L = "nm"
        sq = [A.alloc(L + "sq%d" % k, [128, 512], F32) for k in range(2)]
        rs = [A.alloc(L + "rs%d" % k, [128, 512], F32) for k in range(2)]
        tm = [A.alloc(L + "tm%d" % k, [128, 512], F32) for k in range(2)]
        n_ = 0
        for g, (t0, n) in enumerate(TGS):
            pb = 6 + (g % 2)
            for c in range(DC):
                k = n_ % 2
                n_ += 1
                self.act(sq[k][:, :n], self.XT[:, c, t0:t0 + n], AF.Square, ["XT"], [L + "sq%d" % k])
                self.mm(self.bank(pb, n), self.ones, sq[k][:, :n], c == 0, c == DC - 1, [L + "sq%d" % k, "cst"], [("ps", pb)])
            r = rs[g % 2]
            rk = L + "rs%d" % (g % 2)
            self.act(r[:, :n], self.bank(pb, n), AF.Sqrt, [("ps", pb)], [rk], scale=1.0 / D, bias=self.epsb[:, 0:1])
            self.P.op("dve", lambda r=r, n=n: self.nc.vector.reciprocal(out=r[:, :n], in_=r[:, :n]), [rk], [rk])
            s = 1 if g == 4 else 0
            for c in range(DC):
                k = c % 2
                self.tt("dve", tm[k][:, :n], self.XT[:, c, t0:t0 + n], r[:, :n], ALU.mult, ["XT", rk], [L + "tm%d" % k])
                if g < 4:
                    self.ts("pool", uT[:, c, t0:t0 + n], tm[k][:, :n], self.gsc[:, i, c, 0:1], self.mT[:, (3 * i) * 8 + c, 0:1],
                            ALU.mult, ALU.add, [L + "tm%d" % k, "gsc", "mT"], [(ukey, c, g)])
                else:
                    self.ts("pool", uT[:, c, t0:t0 + n], tm[k][:, :n], self.gsc[:, i, c, 1:2], self.mT[:, (3 * i) * 8 + c, 1:2],
                            ALU.mult, ALU.add, [L + "tm%d" % k, "gsc", "mT"], [(ukey, c, g)])

    def ffn(self, l, f):
        nc, P = self.nc, self.P
        i = 0 if f == 0 else 2
        W = self.W[l]
        A = self.A0.sub()
        L = "f"
        uT = A.alloc(L + "u", [128, DC, T], BF16)
        hg = A.alloc(L + "hg", [128, 4, T], BF16)
        NWI = 3
        wi = [A.alloc(L + "wi%d" % k, [128, DC, 512], BF16) for k in range(NWI)]
        wo = [A.alloc(L + "wo%d" % k, [128, 4, D], BF16) for k in range(2)]
        sa = [A.alloc(L + "sa%d" % k, [128, 512], F32) for k in range(2)]
        self.norm_mod(l, i, uT, L + "u", A)
        w_in = W["fw_in"][f].rearrange("(c p) n -> p c n", p=128)
        w_out = W["fw_out"][f].rearrange("(j p) n -> p j n", p=128)
        j0 = 0
        nsl = 0
        cnt = 0
        jstart = [sum(HBLK[:hb]) for hb in range(len(HBLK))]

        def issue_wo(hb):
            if hb >= len(HBLK):
                return
            wob_, HB_, j0_ = wo[hb % 2], HBLK[hb], jstart[hb]
            P.dma("pool", lambda: nc.gpsimd.dma_start(out=wob_[:, 0:HB_, :], in_=w_out[:, j0_:j0_ + HB_, :]),
                  (L, "wo", hb % 2), [], [L + "wo%d" % (hb % 2)])

        def issue_wi(si):
            if si >= NJ // 2:
                return
            wb_ = wi[si % NWI]
            P.dma("pool", lambda: nc.gpsimd.dma_start(out=wb_[:], in_=w_in[:, :, si * 512:(si + 1) * 512]),
                  (L, "wi", si % NWI), [], [L + "wi%d" % (si % NWI)])
        issue_wo(0)
        issue_wi(0)
        issue_wi(1)
        for hb, HB in enumerate(HBLK):
            wob = wo[hb % 2]
            wok = L + "wo%d" % (hb % 2)
            issue_wo(hb + 1)
            for sl in range(HB // 2):
                jj = j0 + sl * 2
                wb = wi[nsl % NWI]
                wk = L + "wi%d" % (nsl % NWI)
                issue_wi(nsl + 2)
                nsl += 1
                for q in range(2):
                    jl = sl * 2 + q
                    for g, (t0, n) in enumerate(TGS):
                        pa, pb = (0, 1) if cnt % 2 == 0 else (2, 3)
                        k = cnt % 2
                        cnt += 1
                        for c in range(DC):
                            self.mm(self.bank(pa, n), wb[:, c, q * 256:q * 256 + 128], uT[:, c, t0:t0 + n], c == 0, c == DC - 1,
                                    [wk, (L + "u", c, g)], [("ps", pa)])
                        for c in range(DC):
                            self.mm(self.bank(pb, n), wb[:, c, q * 256 + 128:q * 256 + 256], uT[:, c, t0:t0 + n], c == 0, c == DC - 1,
                                    [wk, (L + "u", c, g)], [("ps", pb)])
                        self.act(sa[k][:, :n], self.bank(pa, n), AF.Silu, [("ps", pa)], [L + "sa%d" % k])
                        self.tt("dve", hg[:, jl, t0:t0 + n], sa[k][:, :n], self.bank(pb, n), ALU.mult,
                                [L + "sa%d" % k, ("ps", pb)], [(L + "hg", jl, g)])
            oc_cnt = 0
            for oc in range(DC):
                for g, (t0, n) in enumerate(TGS):
                    pb = 4 + (oc_cnt % 2)
                    oc_cnt += 1
                    for jl in range(HB):
                        self.mm(self.bank(pb, n), wob[:, jl, oc * 128:(oc + 1) * 128], hg[:, jl, t0:t0 + n], jl == 0, jl == HB - 1,
                                [wok, (L + "hg", jl, g)], [("ps", pb)])
                    s = 1 if g == 4 else 0
                    self.stt("dve", self.XT[:, oc, t0:t0 + n], self.bank(pb, n), self.gth[:, i, oc, s:s + 1], self.XT[:, oc, t0:t0 + n],
                             ALU.mult, ALU.add, [("ps", pb), "gth", "XT"], ["XT"])
            j0 += HB

    def inproj(self, l):
        nc, P = self.nc, self.P
        W = self.W[l]
        E = self.X[l]
        A = self.A0.sub()
        L = "p"
        uT = A.alloc(L + "u", [128, DC, T], BF16)
        NWI = 3
        wi = [A.alloc(L + "wi%d" % k, [128, DC, 512], BF16) for k in range(NWI)]
        cs = A.alloc(L + "cs", [128, 2, T], F32)
        qf = [A.alloc(L + "qf%d" % k, [128, 512], F32) for k in range(2)]
        t1 = [A.alloc(L + "t1%d" % k, [128, 512], F32) for k in range(2)]
        t2 = [A.alloc(L + "t2%d" % k, [128, 512], F32) for k in range(2)]
        qo = [A.alloc(L + "qo%d" % k, [128, T], BF16) for k in range(2)]
        vs = [A.alloc(L + "vs%d" % k, [128, 4, VW], BF16) for k in range(2)]
        fs = [A.alloc(L + "fs%d" % k, [128, 512], BF16) for k in range(2)]
        self.load(cs[:], self.cossin, L + "cs", [], [L + "cs"])
        self.norm_mod(l, 1, uT, L + "u", A)
        wv = W["w_in"].rearrange("(c p) n -> p c n", p=128)
        for k in range(2):
            P.op("pool", lambda k=k: nc.gpsimd.memset(vs[k][:], 0.0), [], [L + "vs%d" % k])
            P.op("pool", lambda k=k: nc.gpsimd.memset(vs[k][:, :, 128:129], 1.0), [L + "vs%d" % k], [L + "vs%d" % k])
        cnt = 0
        ocnt = 0
        vcnt = 0
        def issue_wi(si):
            if si >= 11:
                return
            wb_ = wi[si % NWI]
            P.dma("pool", lambda: nc.gpsimd.dma_start(out=wb_[:], in_=wv[:, :, si * 512:(si + 1) * 512]),
                  (L, "wi", si % NWI), [], [L + "wi%d" % (si % NWI)])
        issue_wi(0)
        issue_wi(1)
        for sl in range(11):
            wb = wi[sl % NWI]
            wk = L + "wi%d" % (sl % NWI)
            issue_wi(sl + 2)
            if sl < 4:
                for hc in range(4):
                    h = (sl % 2) * 4 + hc
                    ob = qo[ocnt % 2]
                    ok = L + "qo%d" % (ocnt % 2)
                    ocnt += 1
                    for g, (t0, n) in enumerate(TGS):
                        k = cnt % 2
                        pa, pb = (0, 1) if k == 0 else (2, 3)
                        cnt += 1
                        for c in range(DC):
                            self.mm(self.bank(pa, n), wb[:, c, hc * 128:(hc + 1) * 128], uT[:, c, t0:t0 + n], c == 0, c == DC - 1,
                                    [wk, (L + "u", c, g)], [("ps", pa)])
                        self.copy("act", qf[k][:, :n], self.bank(pa, n), [("ps", pa)], [L + "qf%d" % k])
                        self.mm(self.bank(pb, n), self.RT, qf[k][:, :n], True, True, [L + "qf%d" % k, "cst"], [("ps", pb)])
                        self.tt("dve", t1[k][:, :n], qf[k][:, :n], cs[:, 0, t0:t0 + n], ALU.mult, [L + "qf%d" % k, L + "cs"], [L + "t1%d" % k])
                        self.tt("dve", t2[k][:, :n], self.bank(pb, n), cs[:, 1, t0:t0 + n], ALU.mult, [("ps", pb), L + "cs"], [L + "t2%d" % k])
                        self.tt("pool", ob[:, t0:t0 + n], t1[k][:, :n], t2[k][:, :n], ALU.add, [L + "t1%d" % k, L + "t2%d" % k], [ok])
                    dst = E["qT_d"][h * 128:(h + 1) * 128, :] if sl < 2 else E["kT_d"][h]
                    P.dma("sp", lambda ob=ob, dst=dst: nc.sync.dma_start(out=dst, in_=ob[:]),
                          (L, "qo", (ocnt - 1) % 2), [ok], [(L, "qk_d", sl < 2)])
            elif sl < 7:
                for tt_ in range(NKT):
                    k = cnt % 2
                    pa = 0 if k == 0 else 2
                    cnt += 1
                    g = min(tt_ // 4, 4)
                    for c in range(DC):
                        self.mm(self.bank(pa), uT[:, c, tt_ * 128:(tt_ + 1) * 128], wb[:, c, :], c == 0, c == DC - 1,
                                [wk, (L + "u", c, g)], [("ps", pa)])
                    if sl < 6:
                        vb = vs[vcnt % 2]
                        vk = L + "vs%d" % (vcnt % 2)
                        self.copy("act", vb[:, :, 0:128], self.bank(pa).rearrange("p (h e) -> p h e", e=128), [("ps", pa)], [vk])
                        if tt_ == NKT - 1:
                            P.op("pool", lambda vb=vb: nc.gpsimd.memset(vb[64:128, :, :], 0.0), [vk], [vk])
                        hh = (sl - 4) * 4
                        for h4 in range(4):
                            P.dma("sp", lambda vb=vb, tt_=tt_, hh=hh, h4=h4: nc.sync.dma_start(
                                out=E["v_d"][hh + h4][tt_ * 128:(tt_ + 1) * 128, :], in_=vb[:, h4, :]),
                                (L, "vs", vcnt % 2), [vk], [(L, "v_d")])
                        if tt_ == NKT - 1:
                            P.op("pool", lambda vb=vb: nc.gpsimd.memset(vb[64:128, :, 128:129], 1.0), [vk], [vk])
                        vcnt += 1
                    else:
                        fb = fs[vcnt % 2]
                        fk = L + "fs%d" % (vcnt % 2)
                        self.copy("act", fb[:], self.bank(pa), [("ps", pa)], [fk])
                        for g4 in range(4):
                            P.dma("sp", lambda fb=fb, tt_=tt_, g4=g4: nc.sync.dma_start(out=E["f_d"][g4][tt_ * 128:(tt_ + 1) * 128, :], in_=fb[:, g4 * 128:(g4 + 1) * 128]),
                                  (L, "fs", vcnt % 2), [fk], [(L, "f_d")])
                        vcnt += 1
            else:
                for hc in range(4):
                    gc = (sl - 7) * 4 + hc
                    ob = qo[ocnt % 2]
                    ok = L + "qo%d" % (ocnt % 2)
                    ocnt += 1
                    for g, (t0, n) in enumerate(TGS):
                        k = cnt % 2
                        pa = 0 if k == 0 else 2
                        cnt += 1
                        for c in range(DC):
                            self.mm(self.bank(pa, n), wb[:, c, hc * 128:(hc + 1) * 128], uT[:, c, t0:t0 + n], c == 0, c == DC - 1,
                                    [wk, (L + "u", c, g)], [("ps", pa)])
                        self.act(ob[:, t0:t0 + n], self.bank(pa, n), AF.Sigmoid, [("ps", pa), "bg"], [ok], bias=self.bg[:, gc:gc + 1])
                    P.dma("sp", lambda ob=ob, gc=gc: nc.sync.dma_start(out=E["sg_d"][gc * 128:(gc + 1) * 128, :], in_=ob[:]),
                          (L, "qo", (ocnt - 1) % 2), [ok], [(L, "sg_d")])

    def exchange(self, l):
        nc, P = self.nc, self.P
        E = self.X[l]
        L = "x"
        rg = [[0, 1, 2, 3], [4, 5, 6, 7]]
        for nm, src, dst, rk, cnt in (("k", "kT_d", "kT_g", ("p", "qk_d", False), NH),
                                      ("v", "v_d", "v_g", ("p", "v_d"), NH),
                                      ("f", "f_d", "f_g", ("p", "f_d"), 4)):
            for i in range(cnt):
                P.dma("pool", lambda src=src, dst=dst, i=i: nc.gpsimd.collective_compute(
                    "AllGather", ALU.bypass, replica_groups=rg, ins=[E[src + "32"][i]], outs=[E[dst + "32"][i]]),
                    (L, "cc"), [rk], [(L, dst)], inc=1)

    def attention(self, l):
        nc, P = self.nc, self.P
        E = self.X[l]
        A = self.A0.sub()
        L = "a"
        NKTG = 4 * NKT
        KT = [A.alloc(L + "KT%d" % k, [128, 4 * T], BF16) for k in range(2)]
        V = [A.alloc(L + "V%d" % k, [128, NKTG, VW], BF16) for k in range(2)]
        Q = [A.alloc(L + "Q%d" % k, [128, T], BF16) for k in range(2)]
        NPT = 3
        pT = [A.alloc(L + "pT%d" % k, [128, 2, 512], BF16) for k in range(NPT)]
        on = [A.alloc(L + "on%d" % k, [128, 128], F32) for k in range(2)]
        o1 = [A.alloc(L + "o1%d" % k, [128, 128], F32) for k in range(2)]
        sqj = A.alloc(L + "sqj", [128, 128], F32)
        sm = [A.alloc(L + "sm%d" % k, [128, 8], F32) for k in range(2)]
        aS = [A.alloc(L + "aS%d" % k, [128, 512], BF16) for k in range(2)]
        gx = "x"
        kdep = [(gx, "kT_g")] if self.fused else []
        vdep = [(gx, "v_g")] if self.fused else []
        qdep = [("p", "qk_d", True)]
        scale = 64 ** -0.5
        groups = TGS if l == 0 else TGS[:4]
        groups = groups[:int(os.environ.get("ATT_G", "9"))]
        NHX = min(NH, int(os.environ.get("ATT_H", "9")))
        ptc = 0
        fc = 0
        ac = 0
        def issue_head(h):
            if h >= NH:
                return
            kb_, vb_, qb_ = KT[h % 2], V[h % 2], Q[h % 2]
            kk_, vk_, qk_ = L + "KT%d" % (h % 2), L + "V%d" % (h % 2), L + "Q%d" % (h % 2)
            self.load(qb_[:], E["qT_d"][h * 128:(h + 1) * 128, :], (L, "q", h % 2), qdep, [qk_])
            vg3 = E["v_g"][h].rearrange("(kt p) n -> p kt n", p=128)
            for r in range(4):
                self.load(kb_[:, r * T:(r + 1) * T], E["kT_g"][h][r * 128:(r + 1) * 128, :], (L, "k", h % 2), kdep, [kk_])
            for part in range(4):
                self.load(vb_[:, part * NKT:(part + 1) * NKT, :], vg3[:, part * NKT:(part + 1) * NKT, :],
                          (L, "v", h % 2), vdep, [vk_])
        issue_head(0)
        for h in range(NHX):
            kb, vb, qb = KT[h % 2], V[h % 2], Q[h % 2]
            kk, vk, qk = L + "KT%d" % (h % 2), L + "V%d" % (h % 2), L + "Q%d" % (h % 2)
            issue_head(h + 1)
            for g, (t0, n) in enumerate(groups):
                nsub = n // 128
                kts = list(range(NKTG)) if g < 4 else [NKT - 1 + NKT * r for r in range(4)]
                def acc(c, s):
                    idx = c * 4 + s
                    return self.ps[:, (4 + idx // 3) * 512 + (idx % 3) * 129:(4 + idx // 3) * 512 + (idx % 3) * 129 + 129]
                for ki, kt in enumerate(kts):
                    sb = (ki % 2) * 2
                    pt = pT[ptc % NPT]
                    pk = L + "pT%d" % (ptc % NPT)
                    ptc += 1
                    for c in range(2):
                        self.mm(self.bank(sb + c, n), kb[c * 64:(c + 1) * 64, kt * 128:(kt + 1) * 128], qb[c * 64:(c + 1) * 64, t0:t0 + n],
                                True, True, [kk, qk], [("ps", sb + c)])
                    self.act(pt[:, :, :n], self.ps[:, sb * 512:(sb + 2) * 512].rearrange("p (c n) -> p c n", c=2)[:, :, :n], AF.Exp,
                             [("ps", sb), ("ps", sb + 1)], [pk], scale=scale)
                    for c in range(2):
                        for s in range(nsub):
                            idx = c * 4 + s
                            active = [cc * 4 + ss for cc in range(2) for ss in range(nsub)]
                            first = (ki == 0) and idx == min(a for a in active if a // 3 == idx // 3)
                            self.mm(acc(c, s), pt[:, c, s * 128:(s + 1) * 128], vb[:, kt, 0:129], first, ki == len(kts) - 1,
                                    [pk, vk], [("ps", 4 + idx // 3)], skip_group_check=True)
                ab = aS[ac % 2]
                ak = L + "aS%d" % (ac % 2)
                ac += 1
                for s in range(nsub):
                    k = fc % 2
                    fc += 1
                    smk = L + "sm%d" % k
                    b0 = ("ps", 4 + s // 3)
                    b1 = ("ps", 4 + (4 + s) // 3)
                    a0, a1 = acc(0, s), acc(1, s)
                    if self.debug and h == 0 and g == 0 and s == 0:
                        dacc = A.alloc("dacc", [128, 2, 129], F32)
                        self.copy("dve", dacc[:, 0, :], a0, [b0], ["dacc"])
                        self.copy("dve", dacc[:, 1, :], a1, [b1], ["dacc"])
                        dd = self.dout("dbg_acc", [128, 2, 129])
                        P.dma("sp", lambda: nc.sync.dma_start(out=dd, in_=dacc[:]), "dbg1", ["dacc"], ["dbgacc"])
                    P.op("dve", lambda k=k, a0=a0: nc.vector.reciprocal(out=sm[k][:, 0:1], in_=a0[:, 128:129]), [b0], [smk])
                    P.op("dve", lambda k=k, a1=a1: nc.vector.reciprocal(out=sm[k][:, 1:2], in_=a1[:, 128:129]), [b1], [smk])
                    self.tt("dve", sm[k][:, 2:3], sm[k][:, 1:2], self.lamv[:, 5:6], ALU.mult, [smk, "lamv"], [smk])
                    self.ts("dve", o1[k][:], a1[:, 0:128], sm[k][:, 2:3], None, ALU.mult, None, [b1, smk], [L + "o1%d" % k])
                    self.stt("dve", on[k][:], a0[:, 0:128], sm[k][:, 0:1], o1[k][:], ALU.mult, ALU.add, [b0, smk, L + "o1%d" % k], [L + "on%d" % k])
                    self.tt("dve", sqj[:], on[k][:], on[k][:], ALU.mult, [L + "on%d" % k], [L + "sqj"])
                    P.op("dve", lambda k=k: nc.vector.reduce_sum(out=sm[k][:, 3:4], in_=sqj[:], axis=AX.X), [L + "sqj"], [smk])
                    self.act(sm[k][:, 4:5], sm[k][:, 3:4], AF.Sqrt, [smk], [smk], scale=1.0 / 128, bias=self.epsb[:, 0:1])
                    P.op("dve", lambda k=k: nc.vector.reciprocal(out=sm[k][:, 5:6], in_=sm[k][:, 4:5]), [smk], [smk])
                    self.stt("dve", on[k][:], on[k][:], sm[k][:, 5:6], self.sgb[:], ALU.mult, ALU.mult, [L + "on%d" % k, smk, "sgb"], [L + "on%d" % k])
                    if self.debug and h == 0 and g == 0 and s == 0:
                        dd2 = self.dout("dbg_on", [128, 128])
                        dd3 = self.dout("dbg_sm", [128, 8])
                        dd4 = self.dout("dbg_lamv", [128, 8])
                        dd5 = self.dout("dbg_sgb", [128, 128])
                        P.dma("sp", lambda k=k: nc.sync.dma_start(out=dd2, in_=on[k][:]), "dbg2", [L + "on%d" % k], ["dbgon"])
                        P.dma("sp", lambda k=k: nc.sync.dma_start(out=dd3, in_=sm[k][:]), "dbg3", [smk], ["dbgsm"])
                        P.dma("sp", lambda: nc.sync.dma_start(out=dd4, in_=self.lamv[:]), "dbg4", ["lamv"], ["dbglamv"])
                        P.dma("sp", lambda: nc.sync.dma_start(out=dd5, in_=self.sgb[:]), "dbg5", ["sgb"], ["dbgsgb"])
                    P.op("pe", lambda k=k: nc.tensor.transpose(self.ps[:, 7 * 512 + (k * 128):7 * 512 + (k + 1) * 128], on[k][:], self.ident),
                         [L + "on%d" % k, "cst"], [("ps7", k)])
                    self.copy("dve", ab[:, s * 128:(s + 1) * 128], self.ps[:, 7 * 512 + (k * 128):7 * 512 + (k + 1) * 128], [("ps7", k)], [ak])
                P.dma("sp", lambda ab=ab, h=h, t0=t0, n=n: nc.sync.dma_start(out=E["aT_d"][h * 128:(h + 1) * 128, t0:t0 + n], in_=ab[:, :n]),
                      (L, "aS", (ac - 1) % 2), [ak], [(L, "aT_d")])

    def fourier(self, l):
        nc, P = self.nc, self.P
        E = self.X[l]
        L = "r"
        yoff = SB_END - 4 * T * 2
        yoff = yoff // 64 * 64
        self.YT = nc.alloc_sbuf_tensor_at(L + "YT", [128, 4, T], BF16, offset=yoff)
        A = Arena(nc, self.A0.cur, yoff)
        Xs = A.alloc(L + "X", [128, 64, 512], BF16)
        Xc = A.alloc(L + "Xc", [128, 4, 512], BF16)
        NDB = 3
        NTB = 8
        db = [A.alloc(L + "db%d" % k, [128, NTB, 2, 256], BF16) for k in range(NDB)]
        dc = A.alloc(L + "dc", [128, 4, 2, 256], BF16)
        pq = [A.alloc(L + "pq%d" % k, [128, 8, 256], BF16) for k in range(2)]
        fdep = [("x", "f_g")] if self.fused else []
        for r in range(4):
            for g4 in range(4):
                fg = E["f_g"][g4]
                self.load(Xs[:, r * 16:(r + 1) * 16, g4 * 128:(g4 + 1) * 128], fg[r * T:r * T + TL, :].rearrange("(i p) n -> p i n", p=128),
                          (L, "X"), fdep, [L + "X"])
                self.load(Xc[:, r, g4 * 128:(g4 + 1) * 128], fg[r * T + TL:r * T + TL + 128, :], (L, "Xc"), fdep, [L + "Xc"])
        self.load(dc[:], self.dftC, L + "dc", [], [L + "dc"])
        dcnt = 0
        ycnt = 0
        kgs = list(range(8)) + ([8] if l == 0 else [])
        NB = 64 // NTB

        def issue_db(i):
            if i >= 8 * NB:
                return
            d_ = db[i % NDB]
            self.load(d_[:], self.dftL[i // NB, :, (i % NB) * NTB:(i % NB + 1) * NTB, :, :], (L, "db", i % NDB), [], [L + "db%d" % (i % NDB)])
        issue_db(0)
        issue_db(1)
        for kg in kgs:
            def accp(gi, csi, n=256):
                idx = gi * 2 + csi
                return self.ps[:, (idx // 2) * 512 + (idx % 2) * 256:(idx // 2) * 512 + (idx % 2) * 256 + n]
            if kg < 8:
                for nb in range(64 // NTB):
                    d = db[dcnt % NDB]
                    dk = L + "db%d" % (dcnt % NDB)
                    issue_db(dcnt + 2)
                    dcnt += 1
                    for ni in range(NTB):
                        nt = nb * NTB + ni
                        for gi in range(4):
                            for csi in range(2):
                                idx = gi * 2 + csi
                                first = (nt == 0) and (idx % 2 == 0)
                                if FOU_DBG >= 2:
                                    self.mm(accp(gi, csi), Xs[:, nt, gi * 128:(gi + 1) * 128], d[:, ni, csi, :], first, nt == 63,
                                            [L + "X", dk], [("ps", idx // 2)], skip_group_check=True)
                                else:
                                    self.copy("dve", self.bar[:, 1:2], d[:, ni, csi, 0:1], [dk, L + "X"], ["bar1"])
                ncol = 256
                t0 = kg * 256
            else:
                ncol = 256
                t0 = TL
                for r in range(4 if FOU_DBG != 5 else 0):
                    for gi in range(4):
                        for csi in range(2):
                            idx = gi * 2 + csi
                            first = (r == 0) and (idx % 2 == 0)
                            self.mm(accp(gi, csi), Xc[:, r, gi * 128:(gi + 1) * 128], dc[:, r, csi, :], first, r == 3,
                                    [L + "Xc", L + "dc"], [("ps", idx // 2)], skip_group_check=True)
            if FOU_DBG < 3 or (FOU_DBG == 5 and kg == 8):
                continue
            pb = pq[ycnt % 2]
            pk = L + "pq%d" % (ycnt % 2)
            ycnt += 1
            for b in range(4):
                src = self.ps[:, b * 512:(b + 1) * 512].rearrange("p (i n) -> p i n", i=2)[:, :, :ncol]
                self.copy("act" if b % 2 == 0 else "dve", pb[:, 2 * b:2 * b + 2, :ncol], src, [("ps", b)], [(pk, b)])
            if FOU_DBG < 4:
                continue
            for gi in range(4):
                ybank = 4 + (gi % 2)
                for csi in range(2):
                    self.mm(self.bank(ybank, ncol), self.dch[:, csi, :], pb[:, gi * 2 + csi, :ncol], csi == 0, csi == 1,
                            [(pk, gi), "dch"], [("ps", ybank)])
                nw = ncol if kg < 8 else 128
                self.copy("dve", self.YT[:, gi, t0:t0 + nw], self.bank(ybank, nw), [("ps", ybank)], ["YT"])

    def merge(self, l):
        nc, P = self.nc, self.P
        W = self.W[l]
        E = self.X[l]
        L = "g"
        yoff = (SB_END - 4 * T * 2) // 64 * 64
        A = Arena(nc, self.A0.cur, yoff)
        wao = A.alloc(L + "wao", [128, 8, D], BF16)
        wfo = A.alloc(L + "wfo", [128, 4, D], BF16)
        wo = A.alloc(L + "wo", [128, 8, D], BF16)
        sg = [A.alloc(L + "sg%d" % k, [128, 16, 512], BF16) for k in range(2)]
        aT = [A.alloc(L + "aT%d" % k, [128, 8, 512], BF16) for k in range(2)]
        mg = A.alloc(L + "mg", [128, 8, 512], BF16)
        ta = [A.alloc(L + "ta%d" % k, [128, 512], F32) for k in range(2)]
        tb = [A.alloc(L + "tb%d" % k, [128, 512], F32) for k in range(2)]
        P.dma("pool", lambda: nc.gpsimd.dma_start(out=wao[:], in_=W["w_ao"].rearrange("(c p) n -> p c n", p=128)), L + "w1", [], [L + "wao"])
        P.dma("pool", lambda: nc.gpsimd.dma_start(out=wfo[:], in_=W["w_fo"].rearrange("(c p) n -> p c n", p=128)), L + "w2", [], [L + "wfo"])
        P.dma("pool", lambda: nc.gpsimd.dma_start(out=wo[:], in_=W["w_o"].rearrange("(c p) n -> p c n", p=128)), L + "w3", [], [L + "wo"])
        sg3 = E["sg_d"].rearrange("(c p) t -> p c t", p=128)
        a3 = E["aT_d"].rearrange("(c p) t -> p c t", p=128)
        sgdep = [("p", "sg_d")]
        adep = [("a", "aT_d")]
        groups = TGS if l == 0 else TGS[:4]
        cnt = 0
        for g, (t0, n) in enumerate(groups):
            sb = sg[g % 2]
            sk = L + "sg%d" % (g % 2)
            ab = aT[g % 2]
            ak = L + "aT%d" % (g % 2)
            self.load(sb[:, :, :n], sg3[:, :, t0:t0 + n], (L, "sg", g % 2), sgdep, [sk])
            self.load(ab[:, :, :n], a3[:, :, t0:t0 + n], (L, "aT", g % 2), adep, [ak])
            for oc in range(DC):
                k = cnt % 2
                cnt += 1
                pa, pf = (0, 1) if k == 0 else (2, 3)
                for h in range(NH):
                    self.mm(self.bank(pa, n), wao[:, h, oc * 128:(oc + 1) * 128], ab[:, h, :n], h == 0, h == NH - 1, [L + "wao", ak], [("ps", pa)])
                for gi in range(4):
                    self.mm(self.bank(pf, n), wfo[:, gi, oc * 128:(oc + 1) * 128], self.YT[:, gi, t0:t0 + n], gi == 0, gi == 3, [L + "wfo", "YT"], [("ps", pf)])
                self.tt("dve", ta[k][:, :n], self.bank(pa, n), sb[:, oc, :n], ALU.mult, [("ps", pa), sk], [L + "ta%d" % k])
                self.tt("dve", tb[k][:, :n], self.bank(pf, n), sb[:, 8 + oc, :n], ALU.mult, [("ps", pf), sk], [L + "tb%d" % k])
                self.tt("pool", mg[:, oc, :n], ta[k][:, :n], tb[k][:, :n], ALU.add, [L + "ta%d" % k, L + "tb%d" % k], [(L + "mg", oc)])
            s = 1 if g == 4 else 0
            for oc in range(DC):
                pb = 4 + (oc % 2)
                for c in range(DC):
                    self.mm(self.bank(pb, n), wo[:, c, oc * 128:(oc + 1) * 128], mg[:, c, :n], c == 0, c == DC - 1, [L + "wo", (L + "mg", c)], [("ps", pb)])
                self.stt("dve", self.XT[:, oc, t0:t0 + n], self.bank(pb, n), self.gth[:, 1, oc, s:s + 1], self.XT[:, oc, t0:t0 + n],
                         ALU.mult, ALU.add, [("ps", pb), "gth", "XT"], ["XT"])

    def final_norm(self):
        nc, P = self.nc, self.P
        A = self.A0.sub()
        L = "fin"
        sq = [A.alloc(L + "sq%d" % k, [128, 512], F32) for k in range(2)]
        rs = [A.alloc(L + "rs%d" % k, [128, 512], F32) for k in range(2)]
        ot = [A.alloc(L + "ot%d" % k, [128, DC, 512], F32) for k in range(2)]
        n_ = 0
        for g, (t0, n) in enumerate(TGS[:4]):
            pb = 6 + (g % 2)
            for c in range(DC):
                k = n_ % 2
                n_ += 1
                self.act(sq[k][:, :n], self.XT[:, c, t0:t0 + n], AF.Square, ["XT"], [L + "sq%d" % k])
                self.mm(self.bank(pb, n), self.ones, sq[k][:, :n], c == 0, c == DC - 1, [L + "sq%d" % k, "cst"], [("ps", pb)])
            r = rs[g % 2]
            rk = L + "rs%d" % (g % 2)
            self.act(r[:, :n], self.bank(pb, n), AF.Sqrt, [("ps", pb)], [rk], scale=1.0 / D, bias=self.epsb[:, 0:1])
            P.op("dve", lambda r=r, n=n: nc.vector.reciprocal(out=r[:, :n], in_=r[:, :n]), [rk], [rk])
            o = ot[g % 2]
            okk = L + "ot%d" % (g % 2)
            for c in range(DC):
                self.stt("dve", o[:, c, :n], self.XT[:, c, t0:t0 + n], self.fg[:, c:c + 1], r[:, :n], ALU.mult, ALU.mult, ["XT", "fg", rk], [okk])
            P.dma("sp", lambda o=o, t0=t0, n=n: nc.sync.dma_start(out=self.out_ap[:, :, t0:t0 + n], in_=o[:, :, :n]), (L, "ot", g % 2), [okk], ["out"])


_CONST_CACHE = {}


def _bf(a):
    return np.ascontiguousarray(a.astype(ml_dtypes.bfloat16))


def host_consts(qr):
    if qr in _CONST_CACHE:
        return _CONST_CACHE[qr]
    GRID_W = 64
    AXIS = 32
    inv = 1.0 / (10000.0 ** (np.arange(0, AXIS, 2, dtype=np.float32) / AXIS))
    pos = qr * TL + np.arange(TL)
    row = (pos // GRID_W).astype(np.float32)
    col = (pos % GRID_W).astype(np.float32)
    cos = np.ones((128, T), np.float32)
    sin = np.zeros((128, T), np.float32)
    for cbase in (0, 64):
        for ax, p_ in ((0, row), (1, col)):
            ang = p_[None, :] * inv[:, None]
            ang = np.concatenate([ang, ang], 0)
            sgn = np.concatenate([-np.ones(16), np.ones(16)]).astype(np.float32)[:, None]
            sl = slice(cbase + ax * 32, cbase + ax * 32 + 32)
            cos[sl, :TL] = np.cos(ang)
            sin[sl, :TL] = np.sin(ang) * sgn
    cossin = np.stack([cos, sin], 1).astype(np.float32)
    RT = np.zeros((128, 128), np.float32)
    for m in range(128):
        blk = m // 32 * 32
        o = m - blk
        k = blk + (o + 16) % 32
        RT[k, m] = 1.0
    consts = np.zeros((128, 4, 128), np.float32)
    consts[:, 0, :] = 1.0
    consts[:, 1, :] = RT
    consts[:, 2, :] = np.eye(128, dtype=np.float32)
    consts[:, 3, :] = EPS
    cc = np.arange(128)
    angc = 2 * np.pi * ((cc[:, None] * cc[None, :]) % 128) / 128.0
    dftch = np.stack([np.cos(angc), -np.sin(angc)], 1) / np.sqrt(128.0)
    tabc = np.cos(2 * np.pi * np.arange(SEQ) / SEQ) / np.sqrt(SEQ)
    tabs = np.sin(2 * np.pi * np.arange(SEQ) / SEQ) / np.sqrt(SEQ)
    n = (np.arange(64)[None, :] * 128 + np.arange(128)[:, None]).astype(np.int64)
    dftL = np.empty((8, 128, 64, 2, 256), ml_dtypes.bfloat16)
    for kg in range(8):
        k = qr * TL + kg * 256 + np.arange(256, dtype=np.int64)
        idx = (n[:, :, None] * k[None, None, :]) % SEQ
        dftL[kg, :, :, 0, :] = tabc[idx].astype(ml_dtypes.bfloat16)
        dftL[kg, :, :, 1, :] = tabs[idx].astype(ml_dtypes.bfloat16)
    dftC = np.zeros((128, 4, 2, 256), np.float32)
    nn = np.arange(4)[None, :] * 64 + np.arange(64)[:, None]
    kk = qr * 64 + np.arange(64)
    ang = 2 * np.pi * ((nn[:, :, None] * kk[None, None, :]) % CTX) / CTX
    dftC[:64, :, 0, :64] = np.cos(ang) / np.sqrt(CTX)
    dftC[:64, :, 1, :64] = np.sin(ang) / np.sqrt(CTX)
    out = dict(cossin=cossin, consts=consts, dftch=_bf(dftch), dftL=dftL, dftC=_bf(dftC))
    _CONST_CACHE[qr] = out
    return out


def host_weights(inp, l):
    w = {}
    f32 = np.float32
    w["w_mod%d" % l] = np.ascontiguousarray(inp["w_mod"][l], f32)
    bm = np.asarray(inp["b_mod"][l], f32).reshape(72, 128).T
    w["b_mod%d" % l] = np.ascontiguousarray(np.repeat(bm[:, :, None], 2, 2))
    w["norm_g%d" % l] = np.ascontiguousarray(np.asarray(inp["norm_g"][l], f32).reshape(3, DC, 128).transpose(2, 0, 1))
    fwi = np.asarray(inp["ffn_w_in"][l], f32)
    a = fwi[:, :, :DFF].reshape(2, D, NJ, 1, 128)
    b = fwi[:, :, DFF:].reshape(2, D, NJ, 1, 128)
    w["fw_in%d" % l] = np.ascontiguousarray(np.concatenate([a, b], 3).reshape(2, D, NJ * 256))
    w["fw_out%d" % l] = np.ascontiguousarray(inp["ffn_w_out"][l], f32)
    w["w_in%d" % l] = np.ascontiguousarray(inp["w_in"][l], f32)
    bg = np.asarray(inp["b_gate"][l], f32).reshape(2, DC, 128)
    w["b_gate%d" % l] = np.ascontiguousarray(bg.reshape(16, 128).T)
    w["lam%d" % l] = np.ascontiguousarray(np.broadcast_to(np.asarray(inp["lam_params"][l], f32).reshape(1, 256), (128, 256)))
    w["subln%d" % l] = np.ascontiguousarray(np.broadcast_to(np.asarray(inp["subln_g"][l], f32).reshape(1, 128), (128, 128)))
    w["w_ao%d" % l] = np.ascontiguousarray(inp["w_attn_out"][l], f32)
    w["w_fo%d" % l] = np.ascontiguousarray(inp["w_four_out"][l], f32)
    w["w_o%d" % l] = np.ascontiguousarray(inp["w_o"][l], f32)
    return w


def host_x(inp, r):
    b, qr = r // 4, r % 4
    tok = np.zeros((T, D), np.float32)
    tok[:TL] = inp["x"][b, qr * TL:(qr + 1) * TL]
    tok[TL:TL + TCX] = inp["ctx"][b, qr * TCX:(qr + 1) * TCX]
    return np.ascontiguousarray(tok.T.reshape(DC, 128, T).transpose(1, 0, 2))


def host_c(inp, r):
    b = r // 4
    cc = np.stack([np.asarray(inp["c"][b], np.float32), np.asarray(inp["c_ctx"], np.float32)], 1)
    return np.ascontiguousarray(cc.reshape(DC, 128, 2).transpose(1, 0, 2))


def common_inputs(inp, r, layers):
    m = {}
    m["cT"] = host_c(inp, r)
    m.update(host_consts(r % 4))
    fg = np.asarray(inp["final_g"], np.float32).reshape(DC, 128).T
    m["final_g"] = np.ascontiguousarray(fg)
    return m


def run_stage_set(stages, fused, inp, state, wcache):
    b = Builder(list(stages), fused)
    b.cc_inc = 16
    nc = b.nc
    b.build()
    in_maps = []
    layers = sorted(set(int(s[1]) for s in stages))
    for r in range(8):
        m = common_inputs(inp, r, layers)
        for l in layers:
            m.update(wcache[l])
        for name in b.ins:
            if name in state[r]:
                m[name] = state[r][name]
        m = {k: v for k, v in m.items() if k in b.ins}
        missing = [k for k in b.ins if k not in m]
        assert not missing, missing
        in_maps.append(m)
    res = run_bass_kernel_spmd(nc, in_maps, core_ids=list(range(8)))
    return res.results


def kernel(**inp):
    inp = {k: np.asarray(v) for k, v in inp.items()}
    wcache = {l: host_weights(inp, l) for l in range(DEPTH)}
    state = [dict(xT=host_x(inp, r)) for r in range(8)]
    plan = [["A0", "B0", "A1", "B1"]] if FUSED else [["A0"], ["B0", "A1"], ["B1"]]
    for si, stages in enumerate(plan):
        results = run_stage_set(stages, FUSED, inp, state, wcache)
        new_state = [dict() for _ in range(8)]
        for r in range(8):
            out = results[r]
            if "xT_out" in out:
                new_state[r]["xT"] = out["xT_out"]
            for k in out:
                if k.startswith("qT_d") or k.startswith("sg_d"):
                    new_state[r][k] = out[k]
        for grp in ([0, 1, 2, 3], [4, 5, 6, 7]):
            for k in results[grp[0]]:
                for pre in ("kT_d", "v_d", "f_d"):
                    if k.startswith(pre):
                        g = np.concatenate([results[r][k] for r in grp], 0)
                        for r in grp:
                            new_state[r][k.replace("_d", "_g")] = g
        state = new_state
        last = results
    out = np.empty((2, SEQ, D), np.float32)
    for r in range(8):
        b_, qr = r // 4, r % 4
        oT = last[r]["outT"]
        out[b_, qr * TL:(qr + 1) * TL] = oT.transpose(2, 1, 0).reshape(TL, D)
    return out
```

```python
import math
import numpy as np
import ml_dtypes
import concourse.bass as bass
import concourse.mybir as mybir
from concourse.bass_utils import run_bass_kernel_spmd

F32 = mybir.dt.float32
BF16 = mybir.dt.bfloat16
AF = mybir.ActivationFunctionType
ALU = mybir.AluOpType
AX = mybir.AxisListType

D = 1024
DC = 8
DEPTH = 2
SEQ = 8192
CTX = 256
TL = 2048
TCX = 64
T = 2176
NKT = 17
TGS = [(0, 512), (512, 512), (1024, 512), (1536, 512), (2048, 128)]
DFF = 2816
NJ = 22
HBLK = [4, 4, 4, 4, 4, 2]
EPS = 1e-6
NH = 8
VW = 130
INW = 5632
SB_BASE = 16512
SB_END = 227328

COMPUTE = ("pe", "act", "dve", "pool")
SAME_ENGINE_SYNC = True
import os
FOU_DBG = int(os.environ.get("FOU_DBG", "9"))
FUSED = False


class Op:
    __slots__ = ("eng", "fn", "deps", "is_dma", "sem", "semval", "flag", "incval", "slot")

    def __init__(self, eng, fn, is_dma=False):
        self.eng = eng
        self.fn = fn
        self.deps = []
        self.is_dma = is_dma
        self.sem = None
        self.semval = 0
        self.flag = False
        self.incval = 0


class Prog:
    def __init__(self, nc):
        self.nc = nc
        self.ops = {e: [] for e in ("pe", "act", "dve", "pool", "sp")}
        self.lastw = {}
        self.readers = {}
        self.dma_sems = {}
        self.ctx = []

    def enter(self, cm):
        v = cm.__enter__()
        self.ctx.append(cm)
        return v

    def new_sem(self, name):
        return self.enter(self.nc.semaphore(name))

    def _track(self, op, reads, writes):
        deps = op.deps
        for k in reads:
            w = self.lastw.get(k)
            if w is not None:
                deps.append(w)
        for k in writes:
            w = self.lastw.get(k)
            if w is not None:
                deps.append(w)
            rs = self.readers.get(k)
            if rs:
                deps.extend(rs)
        for k in reads:
            self.readers.setdefault(k, []).append(op)
        for k in writes:
            self.lastw[k] = op
            self.readers[k] = []

    def op(self, eng, fn, reads=(), writes=()):
        o = Op(eng, fn)
        self._track(o, list(reads) + ["PHASE"], writes)
        self.ops[eng].append(o)
        return o

    def dma(self, eng, fn, slot, reads=(), writes=(), inc=16):
        o = Op(eng, fn, is_dma=True)
        self._track(o, list(reads) + ["PHASE"], writes)
        o.slot = slot
        o.deps = [d for d in o.deps if not (d.is_dma and d.slot == slot and d.eng == eng)]
        ent = self.dma_sems.get(slot)
        if ent is None:
            ent = [self.new_sem("d_%d" % len(self.dma_sems)), 0]
            self.dma_sems[slot] = ent
        ent[1] += inc
        o.sem = ent[0]
        o.semval = ent[1]
        o.incval = inc
        self.ops[eng].append(o)
        return o

    def barrier(self, fn):
        o = Op("pool", fn)
        self._track(o, [], ["PHASE"])
        self.ops["pool"].append(o)
        return o

    def finalize(self):
        for e, lst in self.ops.items():
            for o in lst:
                for d in o.deps:
                    if not d.is_dma:
                        if d.eng == o.eng and (d.eng == "pe" or not SAME_ENGINE_SYNC):
                            continue
                        d.flag = True
        self.esem = {}
        for e in COMPUTE:
            cnt = 0
            for o in self.ops[e]:
                if o.is_dma:
                    continue
                if o.flag:
                    cnt += 1
                    o.incval = cnt
            if cnt:
                self.esem[e] = self.new_sem("p_" + e)

    def emit_engine(self, e, engobj):
        waited = {}
        for o in self.ops[e]:
            need = {}
            for d in o.deps:
                if d.is_dma:
                    s, v = d.sem, d.semval
                else:
                    if d.eng == e and (e == "pe" or not SAME_ENGINE_SYNC):
                        continue
                    s, v = self.esem[d.eng], d.incval
                key = id(s)
                if v > need.get(key, (None, 0))[1]:
                    need[key] = (s, v)
            for key, (s, v) in need.items():
                if waited.get(key, 0) >= v:
                    continue
                engobj.wait_ge(s, v)
                waited[key] = v
            ins = o.fn()
            if o.is_dma:
                ins.then_inc(o.sem, o.incval)
            elif o.flag:
                ins.then_inc(self.esem[e], 1)
        sems = {}
        for o in self.ops[e]:
            if o.is_dma:
                sems[id(o.sem)] = (o.sem, max(o.semval, sems.get(id(o.sem), (None, 0))[1]))
        for key, (s, v) in sems.items():
            if waited.get(key, 0) < v:
                engobj.wait_ge(s, v)

    def run_blocks(self):
        nc = self.nc
        self.finalize()
        with nc.Block() as block:
            @block.tensor
            def _(eng):
                self.emit_engine("pe", eng)

            @block.scalar
            def _(eng):
                self.emit_engine("act", eng)

            @block.vector
            def _(eng):
                self.emit_engine("dve", eng)

            @block.gpsimd
            def _(eng):
                self.emit_engine("pool", eng)

            @block.sync
            def _(eng):
                self.emit_engine("sp", eng)

    def close(self):
        for cm in reversed(self.ctx):
            cm.__exit__(None, None, None)
        self.ctx = []


class Arena:
    def __init__(self, nc, start, end):
        self.nc, self.start, self.end, self.cur = nc, start, end, start
        self.n = 0

    def alloc(self, name, shape, dtype):
        esz = 4 if dtype == F32 else 2
        nbytes = int(np.prod(shape[1:])) * esz
        off = (self.cur + 63) // 64 * 64
        assert off + nbytes <= self.end, (name, off, nbytes, self.end)
        self.cur = off + nbytes
        self.n += 1
        return self.nc.alloc_sbuf_tensor_at(name, list(shape), dtype, offset=off)

    def sub(self):
        return Arena(self.nc, self.cur, self.end)


class Builder:
    def __init__(self, stages, fused, debug=False):
        self.stages = stages
        self.fused = fused
        self.debug = debug
        self.nc = nc = bass.Bass("TRN2", target_bir_lowering=False)
        self.P = Prog(nc)
        self.uid = 0
        self.ins = {}
        self.outs = {}

    def din(self, name, shape, dtype=F32):
        t = self.nc.dram_tensor(name, list(shape), dtype, kind="ExternalInput").ap()
        self.ins[name] = t
        return t

    def dout(self, name, shape, dtype=F32):
        t = self.nc.dram_tensor(name, list(shape), dtype, kind="ExternalOutput").ap()
        self.outs[name] = t
        return t

    def dint(self, name, shape, dtype=F32):
        return self.nc.dram_tensor(name, list(shape), dtype, kind="Internal").ap()

    def key(self, s):
        self.uid += 1
        return "%s#%d" % (s, self.uid)

    def mm(self, out, lhsT, rhs, start, stop, reads, writes, **kw):
        nc = self.nc
        self.P.op("pe", lambda: nc.tensor.matmul(out, lhsT=lhsT, rhs=rhs, start=start, stop=stop, **kw), reads, writes)

    def act(self, out, in_, func, reads, writes, **kw):
        nc = self.nc
        self.P.op("act", lambda: nc.scalar.activation(out=out, in_=in_, func=func, **kw), reads, writes)

    def tt(self, eng, out, in0, in1, op, reads, writes):
        nc = self.nc
        e = nc.vector if eng == "dve" else nc.gpsimd
        self.P.op(eng, lambda: e.tensor_tensor(out=out, in0=in0, in1=in1, op=op), reads, writes)

    def ts(self, eng, out, in0, s1, s2, op0, op1, reads, writes):
        nc = self.nc
        e = nc.vector if eng == "dve" else nc.gpsimd
        if op1 is None:
            self.P.op(eng, lambda: e.tensor_scalar(out=out, in0=in0, scalar1=s1, scalar2=None, op0=op0), reads, writes)
        else:
            self.P.op(eng, lambda: e.tensor_scalar(out=out, in0=in0, scalar1=s1, scalar2=s2, op0=op0, op1=op1), reads, writes)

    def stt(self, eng, out, in0, scalar, in1, op0, op1, reads, writes):
        nc = self.nc
        e = nc.vector if eng == "dve" else nc.gpsimd
        self.P.op(eng, lambda: e.scalar_tensor_tensor(out=out, in0=in0, scalar=scalar, in1=in1, op0=op0, op1=op1), reads, writes)

    def copy(self, eng, out, in_, reads, writes):
        nc = self.nc
        if eng == "act":
            self.P.op("act", lambda: nc.scalar.copy(out=out, in_=in_), reads, writes)
        else:
            e = nc.vector if eng == "dve" else nc.gpsimd
            self.P.op(eng, lambda: e.tensor_copy(out=out, in_=in_), reads, writes)

    def load(self, out, in_, slot, reads, writes, q="sp"):
        nc = self.nc
        if q == "sp":
            self.P.dma("sp", lambda: nc.sync.dma_start(out=out, in_=in_), slot, reads, writes)
        else:
            self.P.dma("pool", lambda: nc.gpsimd.dma_start(out=out, in_=in_), slot, reads, writes)

    def build(self):
        nc, P = self.nc, self.P
        st = self.stages
        first_l = int(st[0][1])
        self.xT_in = self.din("xT", [128, DC, T])
        self.cT = self.din("cT", [128, DC, 2])
        self.cossin = self.din("cossin", [128, 2, T])
        self.consts = self.din("consts", [128, 4, 128])
        self.dftch = self.din("dftch", [128, 2, 128], BF16)
        self.dftL = self.din("dftL", [8, 128, 64, 2, 256], BF16)
        self.dftC = self.din("dftC", [128, 4, 2, 256], BF16)
        layers = sorted(set(int(s[1]) for s in st))
        self.W = {}
        for l in layers:
            w = {}
            w["w_mod"] = self.din("w_mod%d" % l, [D, 9 * D])
            w["b_mod"] = self.din("b_mod%d" % l, [128, 72, 2])
            w["norm_g"] = self.din("norm_g%d" % l, [128, 3, DC])
            w["fw_in"] = self.din("fw_in%d" % l, [2, D, NJ * 256])
            w["fw_out"] = self.din("fw_out%d" % l, [2, DFF, D])
            w["w_in"] = self.din("w_in%d" % l, [D, INW])
            w["b_gate"] = self.din("b_gate%d" % l, [128, 16])
            w["lam"] = self.din("lam%d" % l, [128, 256])
            w["subln"] = self.din("subln%d" % l, [128, 128])
            w["w_ao"] = self.din("w_ao%d" % l, [D, D])
            w["w_fo"] = self.din("w_fo%d" % l, [512, D])
            w["w_o"] = self.din("w_o%d" % l, [D, D])
            self.W[l] = w
        self.final_g = self.din("final_g", [128, DC])

        self.X = {}
        for l in layers:
            e = {}
            hasA = ("A%d" % l) in st
            hasB = ("B%d" % l) in st
            def xt(kind, key, name, rows, cols):
                t32 = kind(name, [rows, cols // 2], F32)
                e.setdefault(key + "32", []).append(t32)
                e.setdefault(key, []).append(t32.bitcast(BF16))
            kd = self.dint if self.fused else self.dout
            kg = self.dint if self.fused else self.din
            if self.fused or hasA:
                for h in range(NH):
                    xt(kd, "kT_d", "kT_d%d_%d" % (l, h), 128, T)
                    xt(kd, "v_d", "v_d%d_%d" % (l, h), T, VW)
                for g in range(4):
                    xt(kd, "f_d", "f_d%d_%d" % (l, g), T, 128)
            if self.fused or hasB:
                for h in range(NH):
                    xt(kg, "kT_g", "kT_g%d_%d" % (l, h), 4 * 128, T)
                    xt(kg, "v_g", "v_g%d_%d" % (l, h), 4 * T, VW)
                for g in range(4):
                    xt(kg, "f_g", "f_g%d_%d" % (l, g), 4 * T, 128)
            if self.fused or (hasA and hasB):
                e["qT_d"] = self.dint("qT_d%d" % l, [NH * 128, T], BF16)
                e["sg_d"] = self.dint("sg_d%d" % l, [16 * 128, T], BF16)
            elif hasA:
                e["qT_d"] = self.dout("qT_d%d" % l, [NH * 128, T], BF16)
                e["sg_d"] = self.dout("sg_d%d" % l, [16 * 128, T], BF16)
            else:
                e["qT_d"] = self.din("qT_d%d" % l, [NH * 128, T], BF16)
                e["sg_d"] = self.din("sg_d%d" % l, [16 * 128, T], BF16)
            e["aT_d"] = (self.dout if self.debug else self.dint)("aT_d%d" % l, [NH * 128, T], BF16)
            self.X[l] = e
        last_stage = st[-1]
        if last_stage == "B1":
            self.out_ap = self.dout("outT", [128, DC, TL])
        else:
            self.out_ap = self.dout("xT_out", [128, DC, T])

        A = Arena(nc, SB_BASE, SB_END)
        self.XT = A.alloc("XT", [128, DC, T], F32)
        self.cst = A.alloc("cst", [128, 4, 128], F32)
        self.ones = self.cst[:, 0, :]
        self.RT = self.cst[:, 1, :]
        self.ident = self.cst[:, 2, :]
        self.epsb = self.cst[:, 3, :]
        self.bar = A.alloc("bar", [128, 8], F32)
        self.dch = A.alloc("dch", [128, 2, 128], BF16)
        self.mT = A.alloc("mT", [128, 72, 2], F32)
        self.bm = A.alloc("bm", [128, 72, 2], F32)
        self.ng = A.alloc("ng", [128, 3, DC], F32)
        self.gsc = A.alloc("gsc", [128, 3, DC, 2], F32)
        self.gth = A.alloc("gth", [128, 3, DC, 2], F32)
        self.silc = A.alloc("silc", [128, DC, 2], F32)
        self.bg = A.alloc("bg", [128, 16], F32)
        self.lamt = A.alloc("lamt", [128, 256], F32)
        self.lamv = A.alloc("lamv", [128, 8], F32)
        self.sgb = A.alloc("sgb", [128, 128], F32)
        self.fg = A.alloc("fg", [128, DC], F32)
        self.A0 = A.sub()
        self.ps = nc.alloc_psum_tensor("ps", [128, 4096], F32)

        self.load(self.XT[:], self.xT_in, "ld_x", [], ["XT"])
        self.load(self.cst[:], self.consts, "ld_c", [], ["cst"])
        self.load(self.dch[:], self.dftch, "ld_c2", [], ["dch"])
        self.load(self.fg[:], self.final_g, "ld_c3", [], ["fg"])

        self.barrier()
        for s in st:
            l = int(s[1])
            if s[0] == "A":
                self.phase_mod(l)
                self.barrier()
                self.ffn(l, 0)
                self.barrier()
                self.inproj(l)
                self.barrier()
                if self.fused:
                    self.exchange(l)
                    self.barrier()
            else:
                if first_l == l and st[0] == s:
                    self.phase_mod(l)
                    self.barrier()
                only = getattr(self, "only", ("att", "fou", "mrg", "ffn"))
                if "att" in only:
                    self.attention(l)
                    self.barrier()
                if "fou" in only:
                    self.fourier(l)
                    self.barrier()
                    if self.debug:
                        yd = self.dout("YT_dbg", [128, 4, T], BF16)
                        self.P.dma("sp", lambda: nc.sync.dma_start(out=yd, in_=self.YT[:]), "st_y", ["YT"], ["ydbg"])
                if "mrg" in only:
                    self.merge(l)
                    self.barrier()
                if "ffn" in only:
                    self.ffn(l, 1)
                    self.barrier()
        if last_stage == "B1":
            self.final_norm()
        else:
            nc_ = nc
            self.P.dma("sp", lambda: nc_.sync.dma_start(out=self.out_ap, in_=self.XT[:]), "st_x", ["XT"], ["out"])
        P.run_blocks()
        P.close()
        return nc

    def barrier(self):
        nc = self.nc
        self.P.barrier(lambda: nc.gpsimd.memset(self.bar[:, 0:1], 0.0))

    def bank(self, b, n=512, off=0):
        return self.ps[:, b * 512 + off: b * 512 + off + n]

    def phase_mod(self, l):
        nc, P = self.nc, self.P
        W = self.W[l]
        A = self.A0.sub()
        L = "m"
        wst = [A.alloc("wm%d_%d" % (l, i), [128, DC, 512], F32) for i in range(2)]
        cin = A.alloc("cin%d" % l, [128, DC, 2], F32)
        self.load(cin[:], self.cT, L + "c", [], [L + "cin"])
        self.load(self.bm[:], W["b_mod"], L + "b", ["bm"], ["bm"])
        self.load(self.ng[:], W["norm_g"], L + "g", ["ng"], ["ng"])
        self.load(self.bg[:], W["b_gate"], L + "bg", ["bg"], ["bg"])
        self.load(self.lamt[:], W["lam"], L + "lam", ["lamt"], ["lamt"])
        self.load(self.sgb[:], W["subln"], L + "sg", ["sgb"], ["sgb"])
        self.act(self.silc[:], cin[:], AF.Silu, [L + "cin"], ["silc"])
        wv = W["w_mod"].rearrange("(c p) n -> p c n", p=128)
        NS = 18
        for s in range(NS):
            buf = wst[s % 2]
            bk = "wm%d" % (s % 2)
            self.load(buf[:], wv[:, :, s * 512:(s + 1) * 512], (L, "w", s % 2), [], [bk])
            for jj in range(4):
                j = s * 4 + jj
                for c in range(DC):
                    self.mm(self.ps[:, j * 2:(j + 1) * 2], buf[:, c, jj * 128:(jj + 1) * 128], self.silc[:, c, :],
                            c == 0, c == DC - 1, [bk, "silc"], [("ps", 0)], skip_group_check=True)
        self.tt("dve", self.mT[:], self.ps[:, 0:144].rearrange("p (j s) -> p j s", s=2), self.bm[:], ALU.add,
                [("ps", 0), "bm"], ["mT"])
        for i in range(3):
            for s in range(2):
                sc = self.mT[:, (3 * i + 1) * 8:(3 * i + 2) * 8, s]
                gt = self.mT[:, (3 * i + 2) * 8:(3 * i + 3) * 8, s]
                self.stt("dve", self.gsc[:, i, :, s], sc, 1.0, self.ng[:, i, :], ALU.add, ALU.mult,
                         ["mT", "ng"], ["gsc"])
                self.ts("dve", self.gth[:, i, :, s], gt, 0.5 if i != 1 else 1.0, None, ALU.mult, None, ["mT"], ["gth"])
        lam_init = 0.8 - 0.6 * math.exp(-0.3 * l)
        tmp = A.alloc("lamtmp%d" % l, [128, 128], F32)
        self.tt("dve", tmp[:, 0:64], self.lamt[:, 0:64], self.lamt[:, 64:128], ALU.mult, ["lamt"], [L + "lt"])
        self.tt("dve", tmp[:, 64:128], self.lamt[:, 128:192], self.lamt[:, 192:256], ALU.mult, ["lamt"], [L + "lt"])
        P.op("dve", lambda: nc.vector.reduce_sum(out=self.lamv[:, 0:1], in_=tmp[:, 0:64], axis=AX.X), [L + "lt"], ["lamv"])
        P.op("dve", lambda: nc.vector.reduce_sum(out=self.lamv[:, 1:2], in_=tmp[:, 64:128], axis=AX.X), [L + "lt"], ["lamv"])
        self.act(self.lamv[:, 2:4], self.lamv[:, 0:2], AF.Exp, ["lamv"], ["lamv"])
        self.tt("dve", self.lamv[:, 4:5], self.lamv[:, 3:4], self.lamv[:, 2:3], ALU.subtract, ["lamv"], ["lamv"])
        self.ts("dve", self.lamv[:, 5:6], self.lamv[:, 4:5], -lam_init, None, ALU.add, None, ["lamv"], ["lamv"])
        self.ts("dve", self.sgb[:], self.sgb[:], 1.0 - lam_init, None, ALU.mult, None, ["sgb"], ["sgb"])

    def norm_mod(self, l, i, uT, ukey, A):
        L = "nm"
        sq = [A.alloc(L + "sq%d" % k, [128, 512], F32) for k in range(2)]
        rs = [A.alloc(L + "rs%d" % k, [128, 512], F32) for k in range(2)]
        tm = [A.alloc(L + "tm%d" % k, [128, 512], F32) for k in range(2)]
        n_ = 0
        for g, (t0, n) in enumerate(TGS):
            pb = 6 + (g % 2)
            for c in range(DC):
                k = n_ % 2
                n_ += 1
                self.act(sq[k][:, :n], self.XT[:, c, t0:t0 + n], AF.Square, ["XT"], [L + "sq%d" % k])
                self.mm(self.bank(pb, n), self.ones, sq[k][:, :n], c == 0, c == DC - 1, [L + "sq%d" % k, "cst"], [("ps", pb)])
            r = rs[g % 2]
            rk = L + "rs%d" % (g % 2)
            self.act(r[:, :n], self.bank(pb, n), AF.Sqrt, [("ps", pb)], [rk], scale=1.0 / D, bias=self.epsb[:, 0:1])
            self.P.op("dve", lambda r=r, n=n: self.nc.vector.reciprocal(out=r[:, :n], in_=r[:, :n]), [rk], [rk])
            s = 1 if g == 4 else 0
            for c in range(DC):
                k = c % 2
                self.tt("dve", tm[k][:, :n], self.XT[:, c, t0:t0 + n], r[:, :n], ALU.mult, ["XT", rk], [L + "tm%d" % k])
                if g < 4:
                    self.ts("pool", uT[:, c, t0:t0 + n], tm[k][:, :n], self.gsc[:, i, c, 0:1], self.mT[:, (3 * i) * 8 + c, 0:1],
                            ALU.mult, ALU.add, [L + "tm%d" % k, "gsc", "mT"], [(ukey, c, g)])
                else:
                    self.ts("pool", uT[:, c, t0:t0 + n], tm[k][:, :n], self.gsc[:, i, c, 1:2], self.mT[:, (3 * i) * 8 + c, 1:2],
                            ALU.mult, ALU.add, [L + "tm%d" % k, "gsc", "mT"], [(ukey, c, g)])

    def ffn(self, l, f):
        nc, P = self.nc, self.P
        i = 0 if f == 0 else 2
        W = self.W[l]
        A = self.A0.sub()
        L = "f"
        uT = A.alloc(L + "u", [128, DC, T], BF16)
        hg = A.alloc(L + "hg", [128, 4, T], BF16)
        NWI = 3
        wi = [A.alloc(L + "wi%d" % k, [128, DC, 512], BF16) for k in range(NWI)]
        wo = [A.alloc(L + "wo%d" % k, [128, 4, D], BF16) for k in range(2)]
        sa = [A.alloc(L + "sa%d" % k, [128, 512], F32) for k in range(2)]
        self.norm_mod(l, i, uT, L + "u", A)
        w_in = W["fw_in"][f].rearrange("(c p) n -> p c n", p=128)
        w_out = W["fw_out"][f].rearrange("(j p) n -> p j n", p=128)
        j0 = 0
        nsl = 0
        cnt = 0
        jstart = [sum(HBLK[:hb]) for hb in range(len(HBLK))]

        def issue_wo(hb):
            if hb >= len(HBLK):
                return
            wob_, HB_, j0_ = wo[hb % 2], HBLK[hb], jstart[hb]
            P.dma("pool", lambda: nc.gpsimd.dma_start(out=wob_[:, 0:HB_, :], in_=w_out[:, j0_:j0_ + HB_, :]),
                  (L, "wo", hb % 2), [], [L + "wo%d" % (hb % 2)])

        def issue_wi(si):
            if si >= NJ // 2:
                return
            wb_ = wi[si % NWI]
            P.dma("pool", lambda: nc.gpsimd.dma_start(out=wb_[:], in_=w_in[:, :, si * 512:(si + 1) * 512]),
                  (L, "wi", si % NWI), [], [L + "wi%d" % (si % NWI)])
        issue_wo(0)
        issue_wi(0)
        issue_wi(1)
        for hb, HB in enumerate(HBLK):
            wob = wo[hb % 2]
            wok = L + "wo%d" % (hb % 2)
            issue_wo(hb + 1)
            for sl in range(HB // 2):
                jj = j0 + sl * 2
                wb = wi[nsl % NWI]
                wk = L + "wi%d" % (nsl % NWI)
                issue_wi(nsl + 2)
                nsl += 1
                for q in range(2):
                    jl = sl * 2 + q
                    for g, (t0, n) in enumerate(TGS):
                        pa, pb = (0, 1) if cnt % 2 == 0 else (2, 3)
                        k = cnt % 2
                        cnt += 1
                        for c in range(DC):
                            self.mm(self.bank(pa, n), wb[:, c, q * 256:q * 256 + 128], uT[:, c, t0:t0 + n], c == 0, c == DC - 1,
                                    [wk, (L + "u", c, g)], [("ps", pa)])
                        for c in range(DC):
                            self.mm(self.bank(pb, n), wb[:, c, q * 256 + 128:q * 256 + 256], uT[:, c, t0:t0 + n], c == 0, c == DC - 1,
                                    [wk, (L + "u", c, g)], [("ps", pb)])
                        self.act(sa[k][:, :n], self.bank(pa, n), AF.Silu, [("ps", pa)], [L + "sa%d" % k])
                        self.tt("dve", hg[:, jl, t0:t0 + n], sa[k][:, :n], self.bank(pb, n), ALU.mult,
                                [L + "sa%d" % k, ("ps", pb)], [(L + "hg", jl, g)])
            oc_cnt = 0
            for oc in range(DC):
                for g, (t0, n) in enumerate(TGS):
                    pb = 4 + (oc_cnt % 2)
                    oc_cnt += 1
                    for jl in range(HB):
                        self.mm(self.bank(pb, n), wob[:, jl, oc * 128:(oc + 1) * 128], hg[:, jl, t0:t0 + n], jl == 0, jl == HB - 1,
                                [wok, (L + "hg", jl, g)], [("ps", pb)])
                    s = 1 if g == 4 else 0
                    self.stt("dve", self.XT[:, oc, t0:t0 + n], self.bank(pb, n), self.gth[:, i, oc, s:s + 1], self.XT[:, oc, t0:t0 + n],
                             ALU.mult, ALU.add, [("ps", pb), "gth", "XT"], ["XT"])
            j0 += HB

    def inproj(self, l):
        nc, P = self.nc, self.P
        W = self.W[l]
        E = self.X[l]
        A = self.A0.sub()
        L = "p"
        uT = A.alloc(L + "u", [128, DC, T], BF16)
        NWI = 3
        wi = [A.alloc(L + "wi%d" % k, [128, DC, 512], BF16) for k in range(NWI)]
        cs = A.alloc(L + "cs", [128, 2, T], F32)
        qf = [A.alloc(L + "qf%d" % k, [128, 512], F32) for k in range(2)]
        t1 = [A.alloc(L + "t1%d" % k, [128, 512], F32) for k in range(2)]
        t2 = [A.alloc(L + "t2%d" % k, [128, 512], F32) for k in range(2)]
        qo = [A.alloc(L + "qo%d" % k, [128, T], BF16) for k in range(2)]
        vs = [A.alloc(L + "vs%d" % k, [128, 4, VW], BF16) for k in range(2)]
        fs = [A.alloc(L + "fs%d" % k, [128, 512], BF16) for k in range(2)]
        self.load(cs[:], self.cossin, L + "cs", [], [L + "cs"])
        self.norm_mod(l, 1, uT, L + "u", A)
        wv = W["w_in"].rearrange("(c p) n -> p c n", p=128)
        for k in range(2):
            P.op("pool", lambda k=k: nc.gpsimd.memset(vs[k][:], 0.0), [], [L + "vs%d" % k])
            P.op("pool", lambda k=k: nc.gpsimd.memset(vs[k][:, :, 128:129], 1.0), [L + "vs%d" % k], [L + "vs%d" % k])
        cnt = 0
        ocnt = 0
        vcnt = 0
        def issue_wi(si):
            if si >= 11:
                return
            wb_ = wi[si % NWI]
            P.dma("pool", lambda: nc.gpsimd.dma_start(out=wb_[:], in_=wv[:, :, si * 512:(si + 1) * 512]),
                  (L, "wi", si % NWI), [], [L + "wi%d" % (si % NWI)])
        issue_wi(0)
        issue_wi(1)
        for sl in range(11):
            wb = wi[sl % NWI]
            wk = L + "wi%d" % (sl % NWI)
            issue_wi(sl + 2)
            if sl < 4:
                for hc in range(4):
                    h = (sl % 2) * 4 + hc
                    ob = qo[ocnt % 2]
                    ok = L + "qo%d" % (ocnt % 2)
                    ocnt += 1
                    for g, (t0, n) in enumerate(TGS):
                        k = cnt % 2
                        pa, pb = (0, 1) if k == 0 else (2, 3)
                        cnt += 1
                        for c in range(DC):
                            self.mm(self.bank(pa, n), wb[:, c, hc * 128:(hc + 1) * 128], uT[:, c, t0:t0 + n], c == 0, c == DC - 1,
                                    [wk, (L + "u", c, g)], [("ps", pa)])
                        self.copy("act", qf[k][:, :n], self.bank(pa, n), [("ps", pa)], [L + "qf%d" % k])
                        self.mm(self.bank(pb, n), self.RT, qf[k][:, :n], True, True, [L + "qf%d" % k, "cst"], [("ps", pb)])
                        self.tt("dve", t1[k][:, :n], qf[k][:, :n], cs[:, 0, t0:t0 + n], ALU.mult, [L + "qf%d" % k, L + "cs"], [L + "t1%d" % k])
                        self.tt("dve", t2[k][:, :n], self.bank(pb, n), cs[:, 1, t0:t0 + n], ALU.mult, [("ps", pb), L + "cs"], [L + "t2%d" % k])
                        self.tt("pool", ob[:, t0:t0 + n], t1[k][:, :n], t2[k][:, :n], ALU.add, [L + "t1%d" % k, L + "t2%d" % k], [ok])
                    dst = E["qT_d"][h * 128:(h + 1) * 128, :] if sl < 2 else E["kT_d"][h]
                    P.dma("sp", lambda ob=ob, dst=dst: nc.sync.dma_start(out=dst, in_=ob[:]),
                          (L, "qo", (ocnt - 1) % 2), [ok], [(L, "qk_d", sl < 2)])
            elif sl < 7:
                for tt_ in range(NKT):
                    k = cnt % 2
                    pa = 0 if k == 0 else 2
                    cnt += 1
                    g = min(tt_ // 4, 4)
                    for c in range(DC):
                        self.mm(self.bank(pa), uT[:, c, tt_ * 128:(tt_ + 1) * 128], wb[:, c, :], c == 0, c == DC - 1,
                                [wk, (L + "u", c, g)], [("ps", pa)])
                    if sl < 6:
                        vb = vs[vcnt % 2]
                        vk = L + "vs%d" % (vcnt % 2)
                        self.copy("act", vb[:, :, 0:128], self.bank(pa).rearrange("p (h e) -> p h e", e=128), [("ps", pa)], [vk])
                        if tt_ == NKT - 1:
                            P.op("pool", lambda vb=vb: nc.gpsimd.memset(vb[64:128, :, :], 0.0), [vk], [vk])
                        hh = (sl - 4) * 4
                        for h4 in range(4):
                            P.dma("sp", lambda vb=vb, tt_=tt_, hh=hh, h4=h4: nc.sync.dma_start(
                                out=E["v_d"][hh + h4][tt_ * 128:(tt_ + 1) * 128, :], in_=vb[:, h4, :]),
                                (L, "vs", vcnt % 2), [vk], [(L, "v_d")])
                        if tt_ == NKT - 1:
                            P.op("pool", lambda vb=vb: nc.gpsimd.memset(vb[64:128, :, 128:129], 1.0), [vk], [vk])
                        vcnt += 1
                    else:
                        fb = fs[vcnt % 2]
                        fk = L + "fs%d" % (vcnt % 2)
                        self.copy("act", fb[:], self.bank(pa), [("ps", pa)], [fk])
                        for g4 in range(4):
                            P.dma("sp", lambda fb=fb, tt_=tt_, g4=g4: nc.sync.dma_start(out=E["f_d"][g4][tt_ * 128:(tt_ + 1) * 128, :], in_=fb[:, g4 * 128:(g4 + 1) * 128]),
                                  (L, "fs", vcnt % 2), [fk], [(L, "f_d")])
                        vcnt += 1
            else:
                for hc in range(4):
                    gc = (sl - 7) * 4 + hc
                    ob = qo[ocnt % 2]
                    ok = L + "qo%d" % (ocnt % 2)
                    ocnt += 1
                    for g, (t0, n) in enumerate(TGS):
                        k = cnt % 2
                        pa = 0 if k == 0 else 2
                        cnt += 1
                        for c in range(DC):
                            self.mm(self.bank(pa, n), wb[:, c, hc * 128:(hc + 1) * 128], uT[:, c, t0:t0 + n], c == 0, c == DC - 1,
                                    [wk, (L + "u", c, g)], [("ps", pa)])
                        self.act(ob[:, t0:t0 + n], self.bank(pa, n), AF.Sigmoid, [("ps", pa), "bg"], [ok], bias=self.bg[:, gc:gc + 1])
                    P.dma("sp", lambda ob=ob, gc=gc: nc.sync.dma_start(out=E["sg_d"][gc * 128:(gc + 1) * 128, :], in_=ob[:]),
                          (L, "qo", (ocnt - 1) % 2), [ok], [(L, "sg_d")])

    def exchange(self, l):
        nc, P = self.nc, self.P
        E = self.X[l]
        L = "x"
        rg = [[0, 1, 2, 3], [4, 5, 6, 7]]
        for nm, src, dst, rk, cnt in (("k", "kT_d", "kT_g", ("p", "qk_d", False), NH),
                                      ("v", "v_d", "v_g", ("p", "v_d"), NH),
                                      ("f", "f_d", "f_g", ("p", "f_d"), 4)):
            for i in range(cnt):
                P.dma("pool", lambda src=src, dst=dst, i=i: nc.gpsimd.collective_compute(
                    "AllGather", ALU.bypass, replica_groups=rg, ins=[E[src + "32"][i]], outs=[E[dst + "32"][i]]),
                    (L, "cc"), [rk], [(L, dst)], inc=1)

    def attention(self, l):
        nc, P = self.nc, self.P
        E = self.X[l]
        A = self.A0.sub()
        L = "a"
        NKTG = 4 * NKT
        KT = [A.alloc(L + "KT%d" % k, [128, 4 * T], BF16) for k in range(2)]
        V = [A.alloc(L + "V%d" % k, [128, NKTG, VW], BF16) for k in range(2)]
        Q = [A.alloc(L + "Q%d" % k, [128, T], BF16) for k in range(2)]
        NPT = 3
        pT = [A.alloc(L + "pT%d" % k, [128, 2, 512], BF16) for k in range(NPT)]
        on = [A.alloc(L + "on%d" % k, [128, 128], F32) for k in range(2)]
        o1 = [A.alloc(L + "o1%d" % k, [128, 128], F32) for k in range(2)]
        sqj = A.alloc(L + "sqj", [128, 128], F32)
        sm = [A.alloc(L + "sm%d" % k, [128, 8], F32) for k in range(2)]
        aS = [A.alloc(L + "aS%d" % k, [128, 512], BF16) for k in range(2)]
        gx = "x"
        kdep = [(gx, "kT_g")] if self.fused else []
        vdep = [(gx, "v_g")] if self.fused else []
        qdep = [("p", "qk_d", True)]
        scale = 64 ** -0.5
        groups = TGS if l == 0 else TGS[:4]
        groups = groups[:int(os.environ.get("ATT_G", "9"))]
        NHX = min(NH, int(os.environ.get("ATT_H", "9")))
        ptc = 0
        fc = 0
        ac = 0
        def issue_head(h):
            if h >= NH:
                return
            kb_, vb_, qb_ = KT[h % 2], V[h % 2], Q[h % 2]
            kk_, vk_, qk_ = L + "KT%d" % (h % 2), L + "V%d" % (h % 2), L + "Q%d" % (h % 2)
            self.load(qb_[:], E["qT_d"][h * 128:(h + 1) * 128, :], (L, "q", h % 2), qdep, [qk_])
            vg3 = E["v_g"][h].rearrange("(kt p) n -> p kt n", p=128)
            for r in range(4):
                self.load(kb_[:, r * T:(r + 1) * T], E["kT_g"][h][r * 128:(r + 1) * 128, :], (L, "k", h % 2), kdep, [kk_])
            for part in range(4):
                self.load(vb_[:, part * NKT:(part + 1) * NKT, :], vg3[:, part * NKT:(part + 1) * NKT, :],
                          (L, "v", h % 2), vdep, [vk_])
        issue_head(0)
        for h in range(NHX):
            kb, vb, qb = KT[h % 2], V[h % 2], Q[h % 2]
            kk, vk, qk = L + "KT%d" % (h % 2), L + "V%d" % (h % 2), L + "Q%d" % (h % 2)
            issue_head(h + 1)
            for g, (t0, n) in enumerate(groups):
                nsub = n // 128
                kts = list(range(NKTG)) if g < 4 else [NKT - 1 + NKT * r for r in range(4)]
                def acc(c, s):
                    idx = c * 4 + s
                    return self.ps[:, (4 + idx // 3) * 512 + (idx % 3) * 129:(4 + idx // 3) * 512 + (idx % 3) * 129 + 129]
                for ki, kt in enumerate(kts):
                    sb = (ki % 2) * 2
                    pt = pT[ptc % NPT]
                    pk = L + "pT%d" % (ptc % NPT)
                    ptc += 1
                    for c in range(2):
                        self.mm(self.bank(sb + c, n), kb[c * 64:(c + 1) * 64, kt * 128:(kt + 1) * 128], qb[c * 64:(c + 1) * 64, t0:t0 + n],
                                True, True, [kk, qk], [("ps", sb + c)])
                    self.act(pt[:, :, :n], self.ps[:, sb * 512:(sb + 2) * 512].rearrange("p (c n) -> p c n", c=2)[:, :, :n], AF.Exp,
                             [("ps", sb), ("ps", sb + 1)], [pk], scale=scale)
                    for c in range(2):
                        for s in range(nsub):
                            idx = c * 4 + s
                            active = [cc * 4 + ss for cc in range(2) for ss in range(nsub)]
                            first = (ki == 0) and idx == min(a for a in active if a // 3 == idx // 3)
                            self.mm(acc(c, s), pt[:, c, s * 128:(s + 1) * 128], vb[:, kt, 0:129], first, ki == len(kts) - 1,
                                    [pk, vk], [("ps", 4 + idx // 3)], skip_group_check=True)
                ab = aS[ac % 2]
                ak = L + "aS%d" % (ac % 2)
                ac += 1
                for s in range(nsub):
                    k = fc % 2
                    fc += 1
                    smk = L + "sm%d" % k
                    b0 = ("ps", 4 + s // 3)
                    b1 = ("ps", 4 + (4 + s) // 3)
                    a0, a1 = acc(0, s), acc(1, s)
                    if self.debug and h == 0 and g == 0 and s == 0:
                        dacc = A.alloc("dacc", [128, 2, 129], F32)
                        self.copy("dve", dacc[:, 0, :], a0, [b0], ["dacc"])
                        self.copy("dve", dacc[:, 1, :], a1, [b1], ["dacc"])
                        dd = self.dout("dbg_acc", [128, 2, 129])
                        P.dma("sp", lambda: nc.sync.dma_start(out=dd, in_=dacc[:]), "dbg1", ["dacc"], ["dbgacc"])
                    P.op("dve", lambda k=k, a0=a0: nc.vector.reciprocal(out=sm[k][:, 0:1], in_=a0[:, 128:129]), [b0], [smk])
                    P.op("dve", lambda k=k, a1=a1: nc.vector.reciprocal(out=sm[k][:, 1:2], in_=a1[:, 128:129]), [b1], [smk])
                    self.tt("dve", sm[k][:, 2:3], sm[k][:, 1:2], self.lamv[:, 5:6], ALU.mult, [smk, "lamv"], [smk])
                    self.ts("dve", o1[k][:], a1[:, 0:128], sm[k][:, 2:3], None, ALU.mult, None, [b1, smk], [L + "o1%d" % k])
                    self.stt("dve", on[k][:], a0[:, 0:128], sm[k][:, 0:1], o1[k][:], ALU.mult, ALU.add, [b0, smk, L + "o1%d" % k], [L + "on%d" % k])
                    self.tt("dve", sqj[:], on[k][:], on[k][:], ALU.mult, [L + "on%d" % k], [L + "sqj"])
                    P.op("dve", lambda k=k: nc.vector.reduce_sum(out=sm[k][:, 3:4], in_=sqj[:], axis=AX.X), [L + "sqj"], [smk])
                    self.act(sm[k][:, 4:5], sm[k][:, 3:4], AF.Sqrt, [smk], [smk], scale=1.0 / 128, bias=self.epsb[:, 0:1])
                    P.op("dve", lambda k=k: nc.vector.reciprocal(out=sm[k][:, 5:6], in_=sm[k][:, 4:5]), [smk], [smk])
                    self.stt("dve", on[k][:], on[k][:], sm[k][:, 5:6], self.sgb[:], ALU.mult, ALU.mult, [L + "on%d" % k, smk, "sgb"], [L + "on%d" % k])
                    if self.debug and h == 0 and g == 0 and s == 0:
                        dd2 = self.dout("dbg_on", [128, 128])
                        dd3 = self.dout("dbg_sm", [128, 8])
                        dd4 = self.dout("dbg_lamv", [128, 8])
                        dd5 = self.dout("dbg_sgb", [128, 128])
                        P.dma("sp", lambda k=k: nc.sync.dma_start(out=dd2, in_=on[k][:]), "dbg2", [L + "on%d" % k], ["dbgon"])
                        P.dma("sp", lambda k=k: nc.sync.dma_start(out=dd3, in_=sm[k][:]), "dbg3", [smk], ["dbgsm"])
                        P.dma("sp", lambda: nc.sync.dma_start(out=dd4, in_=self.lamv[:]), "dbg4", ["lamv"], ["dbglamv"])
                        P.dma("sp", lambda: nc.sync.dma_start(out=dd5, in_=self.sgb[:]), "dbg5", ["sgb"], ["dbgsgb"])
                    P.op("pe", lambda k=k: nc.tensor.transpose(self.ps[:, 7 * 512 + (k * 128):7 * 512 + (k + 1) * 128], on[k][:], self.ident),
                         [L + "on%d" % k, "cst"], [("ps7", k)])
                    self.copy("dve", ab[:, s * 128:(s + 1) * 128], self.ps[:, 7 * 512 + (k * 128):7 * 512 + (k + 1) * 128], [("ps7", k)], [ak])
                P.dma("sp", lambda ab=ab, h=h, t0=t0, n=n: nc.sync.dma_start(out=E["aT_d"][h * 128:(h + 1) * 128, t0:t0 + n], in_=ab[:, :n]),
                      (L, "aS", (ac - 1) % 2), [ak], [(L, "aT_d")])

    def fourier(self, l):
        nc, P = self.nc, self.P
        E = self.X[l]
        L = "r"
        yoff = SB_END - 4 * T * 2
        yoff = yoff // 64 * 64
        self.YT = nc.alloc_sbuf_tensor_at(L + "YT", [128, 4, T], BF16, offset=yoff)
        A = Arena(nc, self.A0.cur, yoff)
        Xs = A.alloc(L + "X", [128, 64, 512], BF16)
        Xc = A.alloc(L + "Xc", [128, 4, 512], BF16)
        NDB = 3
        NTB = 8
        db = [A.alloc(L + "db%d" % k, [128, NTB, 2, 256], BF16) for k in range(NDB)]
        dc = A.alloc(L + "dc", [128, 4, 2, 256], BF16)
        pq = [A.alloc(L + "pq%d" % k, [128, 8, 256], BF16) for k in range(2)]
        fdep = [("x", "f_g")] if self.fused else []
        for r in range(4):
            for g4 in range(4):
                fg = E["f_g"][g4]
                self.load(Xs[:, r * 16:(r + 1) * 16, g4 * 128:(g4 + 1) * 128], fg[r * T:r * T + TL, :].rearrange("(i p) n -> p i n", p=128),
                          (L, "X"), fdep, [L + "X"])
                self.load(Xc[:, r, g4 * 128:(g4 + 1) * 128], fg[r * T + TL:r * T + TL + 128, :], (L, "Xc"), fdep, [L + "Xc"])
        self.load(dc[:], self.dftC, L + "dc", [], [L + "dc"])
        dcnt = 0
        ycnt = 0
        kgs = list(range(8)) + ([8] if l == 0 else [])
        NB = 64 // NTB

        def issue_db(i):
            if i >= 8 * NB:
                return
            d_ = db[i % NDB]
            self.load(d_[:], self.dftL[i // NB, :, (i % NB) * NTB:(i % NB + 1) * NTB, :, :], (L, "db", i % NDB), [], [L + "db%d" % (i % NDB)])
        issue_db(0)
        issue_db(1)
        for kg in kgs:
            def accp(gi, csi, n=256):
                idx = gi * 2 + csi
                return self.ps[:, (idx // 2) * 512 + (idx % 2) * 256:(idx // 2) * 512 + (idx % 2) * 256 + n]
            if kg < 8:
                for nb in range(64 // NTB):
                    d = db[dcnt % NDB]
                    dk = L + "db%d" % (dcnt % NDB)
                    issue_db(dcnt + 2)
                    dcnt += 1
                    for ni in range(NTB):
                        nt = nb * NTB + ni
                        for gi in range(4):
                            for csi in range(2):
                                idx = gi * 2 + csi
                                first = (nt == 0) and (idx % 2 == 0)
                                if FOU_DBG >= 2:
                                    self.mm(accp(gi, csi), Xs[:, nt, gi * 128:(gi + 1) * 128], d[:, ni, csi, :], first, nt == 63,
                                            [L + "X", dk], [("ps", idx // 2)], skip_group_check=True)
                                else:
                                    self.copy("dve", self.bar[:, 1:2], d[:, ni, csi, 0:1], [dk, L + "X"], ["bar1"])
                ncol = 256
                t0 = kg * 256
            else:
                ncol = 256
                t0 = TL
                for r in range(4 if FOU_DBG != 5 else 0):
                    for gi in range(4):
                        for csi in range(2):
                            idx = gi * 2 + csi
                            first = (r == 0) and (idx % 2 == 0)
                            self.mm(accp(gi, csi), Xc[:, r, gi * 128:(gi + 1) * 128], dc[:, r, csi, :], first, r == 3,
                                    [L + "Xc", L + "dc"], [("ps", idx // 2)], skip_group_check=True)
            if FOU_DBG < 3 or (FOU_DBG == 5 and kg == 8):
                continue
            pb = pq[ycnt % 2]
            pk = L + "pq%d" % (ycnt % 2)
            ycnt += 1
            for b in range(4):
                src = self.ps[:, b * 512:(b + 1) * 512].rearrange("p (i n) -> p i n", i=2)[:, :, :ncol]
                self.copy("act" if b % 2 == 0 else "dve", pb[:, 2 * b:2 * b + 2, :ncol], src, [("ps", b)], [(pk, b)])
            if FOU_DBG < 4:
                continue
            for gi in range(4):
                ybank = 4 + (gi % 2)
                for csi in range(2):
                    self.mm(self.bank(ybank, ncol), self.dch[:, csi, :], pb[:, gi * 2 + csi, :ncol], csi == 0, csi == 1,
                            [(pk, gi), "dch"], [("ps", ybank)])
                nw = ncol if kg < 8 else 128
                self.copy("dve", self.YT[:, gi, t0:t0 + nw], self.bank(ybank, nw), [("ps", ybank)], ["YT"])

    def merge(self, l):
        nc, P = self.nc, self.P
        W = self.W[l]
        E = self.X[l]
        L = "g"
        yoff = (SB_END - 4 * T * 2) // 64 * 64
        A = Arena(nc, self.A0.cur, yoff)
        wao = A.alloc(L + "wao", [128, 8, D], BF16)
        wfo = A.alloc(L + "wfo", [128, 4, D], BF16)
        wo = A.alloc(L + "wo", [128, 8, D], BF16)
        sg = [A.alloc(L + "sg%d" % k, [128, 16, 512], BF16) for k in range(2)]
        aT = [A.alloc(L + "aT%d" % k, [128, 8, 512], BF16) for k in range(2)]
        mg = A.alloc(L + "mg", [128, 8, 512], BF16)
        ta = [A.alloc(L + "ta%d" % k, [128, 512], F32) for k in range(2)]
        tb = [A.alloc(L + "tb%d" % k, [128, 512], F32) for k in range(2)]
        P.dma("pool", lambda: nc.gpsimd.dma_start(out=wao[:], in_=W["w_ao"].rearrange("(c p) n -> p c n", p=128)), L + "w1", [], [L + "wao"])
        P.dma("pool", lambda: nc.gpsimd.dma_start(out=wfo[:], in_=W["w_fo"].rearrange("(c p) n -> p c n", p=128)), L + "w2", [], [L + "wfo"])
        P.dma("pool", lambda: nc.gpsimd.dma_start(out=wo[:], in_=W["w_o"].rearrange("(c p) n -> p c n", p=128)), L + "w3", [], [L + "wo"])
        sg3 = E["sg_d"].rearrange("(c p) t -> p c t", p=128)
        a3 = E["aT_d"].rearrange("(c p) t -> p c t", p=128)
        sgdep = [("p", "sg_d")]
        adep = [("a", "aT_d")]
        groups = TGS if l == 0 else TGS[:4]
        cnt = 0
        for g, (t0, n) in enumerate(groups):
            sb = sg[g % 2]
            sk = L + "sg%d" % (g % 2)
            ab = aT[g % 2]
            ak = L + "aT%d" % (g % 2)
            self.load(sb[:, :, :n], sg3[:, :, t0:t0 + n], (L, "sg", g % 2), sgdep, [sk])
            self.load(ab[:, :, :n], a3[:, :, t0:t0 + n], (L, "aT", g % 2), adep, [ak])
            for oc in range(DC):
                k = cnt % 2
                cnt += 1
                pa, pf = (0, 1) if k == 0 else (2, 3)
                for h in range(NH):
                    self.mm(self.bank(pa, n), wao[:, h, oc * 128:(oc + 1) * 128], ab[:, h, :n], h == 0, h == NH - 1, [L + "wao", ak], [("ps", pa)])
                for gi in range(4):
                    self.mm(self.bank(pf, n), wfo[:, gi, oc * 128:(oc + 1) * 128], self.YT[:, gi, t0:t0 + n], gi == 0, gi == 3, [L + "wfo", "YT"], [("ps", pf)])
                self.tt("dve", ta[k][:, :n], self.bank(pa, n), sb[:, oc, :n], ALU.mult, [("ps", pa), sk], [L + "ta%d" % k])
                self.tt("dve", tb[k][:, :n], self.bank(pf, n), sb[:, 8 + oc, :n], ALU.mult, [("ps", pf), sk], [L + "tb%d" % k])
                self.tt("pool", mg[:, oc, :n], ta[k][:, :n], tb[k][:, :n], ALU.add, [L + "ta%d" % k, L + "tb%d" % k], [(L + "mg", oc)])
            s = 1 if g == 4 else 0
            for oc in range(DC):
                pb = 4 + (oc % 2)
                for c in range(DC):
                    self.mm(self.bank(pb, n), wo[:, c, oc * 128:(oc + 1) * 128], mg[:, c, :n], c == 0, c == DC - 1, [L + "wo", (L + "mg", c)], [("ps", pb)])
                self.stt("dve", self.XT[:, oc, t0:t0 + n], self.bank(pb, n), self.gth[:, 1, oc, s:s + 1], self.XT[:, oc, t0:t0 + n],
                         ALU.mult, ALU.add, [("ps", pb), "gth", "XT"], ["XT"])

    def final_norm(self):
        nc, P = self.nc, self.P
        A = self.A0.sub()
        L = "fin"
        sq = [A.alloc(L + "sq%d" % k, [128, 512], F32) for k in range(2)]
        rs = [A.alloc(L + "rs%d" % k, [128, 512], F32) for k in range(2)]
        ot = [A.alloc(L + "ot%d" % k, [128, DC, 512], F32) for k in range(2)]
        n_ = 0
        for g, (t0, n) in enumerate(TGS[:4]):
            pb = 6 + (g % 2)
            for c in range(DC):
                k = n_ % 2
                n_ += 1
                self.act(sq[k][:, :n], self.XT[:, c, t0:t0 + n], AF.Square, ["XT"], [L + "sq%d" % k])
                self.mm(self.bank(pb, n), self.ones, sq[k][:, :n], c == 0, c == DC - 1, [L + "sq%d" % k, "cst"], [("ps", pb)])
            r = rs[g % 2]
            rk = L + "rs%d" % (g % 2)
            self.act(r[:, :n], self.bank(pb, n), AF.Sqrt, [("ps", pb)], [rk], scale=1.0 / D, bias=self.epsb[:, 0:1])
            P.op("dve", lambda r=r, n=n: nc.vector.reciprocal(out=r[:, :n], in_=r[:, :n]), [rk], [rk])
            o = ot[g % 2]
            okk = L + "ot%d" % (g % 2)
            for c in range(DC):
                self.stt("dve", o[:, c, :n], self.XT[:, c, t0:t0 + n], self.fg[:, c:c + 1], r[:, :n], ALU.mult, ALU.mult, ["XT", "fg", rk], [okk])
            P.dma("sp", lambda o=o, t0=t0, n=n: nc.sync.dma_start(out=self.out_ap[:, :, t0:t0 + n], in_=o[:, :, :n]), (L, "ot", g % 2), [okk], ["out"])


_CONST_CACHE = {}


def _bf(a):
    return np.ascontiguousarray(a.astype(ml_dtypes.bfloat16))


def host_consts(qr):
    if qr in _CONST_CACHE:
        return _CONST_CACHE[qr]
    GRID_W = 64
    AXIS = 32
    inv = 1.0 / (10000.0 ** (np.arange(0, AXIS, 2, dtype=np.float32) / AXIS))
    pos = qr * TL + np.arange(TL)
    row = (pos // GRID_W).astype(np.float32)
    col = (pos % GRID_W).astype(np.float32)
    cos = np.ones((128, T), np.float32)
    sin = np.zeros((128, T), np.float32)
    for cbase in (0, 64):
        for ax, p_ in ((0, row), (1, col)):
            ang = p_[None, :] * inv[:, None]
            ang = np.concatenate([ang, ang], 0)
            sgn = np.concatenate([-np.ones(16), np.ones(16)]).astype(np.float32)[:, None]
            sl = slice(cbase + ax * 32, cbase + ax * 32 + 32)
            cos[sl, :TL] = np.cos(ang)
            sin[sl, :TL] = np.sin(ang) * sgn
    cossin = np.stack([cos, sin], 1).astype(np.float32)
    RT = np.zeros((128, 128), np.float32)
    for m in range(128):
        blk = m // 32 * 32
        o = m - blk
        k = blk + (o + 16) % 32
        RT[k, m] = 1.0
    consts = np.zeros((128, 4, 128), np.float32)
    consts[:, 0, :] = 1.0
    consts[:, 1, :] = RT
    consts[:, 2, :] = np.eye(128, dtype=np.float32)
    consts[:, 3, :] = EPS
    cc = np.arange(128)
    angc = 2 * np.pi * ((cc[:, None] * cc[None, :]) % 128) / 128.0
    dftch = np.stack([np.cos(angc), -np.sin(angc)], 1) / np.sqrt(128.0)
    tabc = np.cos(2 * np.pi * np.arange(SEQ) / SEQ) / np.sqrt(SEQ)
    tabs = np.sin(2 * np.pi * np.arange(SEQ) / SEQ) / np.sqrt(SEQ)
    n = (np.arange(64)[None, :] * 128 + np.arange(128)[:, None]).astype(np.int64)
    dftL = np.empty((8, 128, 64, 2, 256), ml_dtypes.bfloat16)
    for kg in range(8):
        k = qr * TL + kg * 256 + np.arange(256, dtype=np.int64)
        idx = (n[:, :, None] * k[None, None, :]) % SEQ
        dftL[kg, :, :, 0, :] = tabc[idx].astype(ml_dtypes.bfloat16)
        dftL[kg, :, :, 1, :] = tabs[idx].astype(ml_dtypes.bfloat16)
    dftC = np.zeros((128, 4, 2, 256), np.float32)
    nn = np.arange(4)[None, :] * 64 + np.arange(64)[:, None]
    kk = qr * 64 + np.arange(64)
    ang = 2 * np.pi * ((nn[:, :, None] * kk[None, None, :]) % CTX) / CTX
    dftC[:64, :, 0, :64] = np.cos(ang) / np.sqrt(CTX)
    dftC[:64, :, 1, :64] = np.sin(ang) / np.sqrt(CTX)
    out = dict(cossin=cossin, consts=consts, dftch=_bf(dftch), dftL=dftL, dftC=_bf(dftC))
    _CONST_CACHE[qr] = out
    return out


def host_weights(inp, l):
    w = {}
    f32 = np.float32
    w["w_mod%d" % l] = np.ascontiguousarray(inp["w_mod"][l], f32)
    bm = np.asarray(inp["b_mod"][l], f32).reshape(72, 128).T
    w["b_mod%d" % l] = np.ascontiguousarray(np.repeat(bm[:, :, None], 2, 2))
    w["norm_g%d" % l] = np.ascontiguousarray(np.asarray(inp["norm_g"][l], f32).reshape(3, DC, 128).transpose(2, 0, 1))
    fwi = np.asarray(inp["ffn_w_in"][l], f32)
    a = fwi[:, :, :DFF].reshape(2, D, NJ, 1, 128)
    b = fwi[:, :, DFF:].reshape(2, D, NJ, 1, 128)
    w["fw_in%d" % l] = np.ascontiguousarray(np.concatenate([a, b], 3).reshape(2, D, NJ * 256))
    w["fw_out%d" % l] = np.ascontiguousarray(inp["ffn_w_out"][l], f32)
    w["w_in%d" % l] = np.ascontiguousarray(inp["w_in"][l], f32)
    bg = np.asarray(inp["b_gate"][l], f32).reshape(2, DC, 128)
    w["b_gate%d" % l] = np.ascontiguousarray(bg.reshape(16, 128).T)
    w["lam%d" % l] = np.ascontiguousarray(np.broadcast_to(np.asarray(inp["lam_params"][l], f32).reshape(1, 256), (128, 256)))
    w["subln%d" % l] = np.ascontiguousarray(np.broadcast_to(np.asarray(inp["subln_g"][l], f32).reshape(1, 128), (128, 128)))
    w["w_ao%d" % l] = np.ascontiguousarray(inp["w_attn_out"][l], f32)
    w["w_fo%d" % l] = np.ascontiguousarray(inp["w_four_out"][l], f32)
    w["w_o%d" % l] = np.ascontiguousarray(inp["w_o"][l], f32)
    return w


def host_x(inp, r):
    b, qr = r // 4, r % 4
    tok = np.zeros((T, D), np.float32)
    tok[:TL] = inp["x"][b, qr * TL:(qr + 1) * TL]
    tok[TL:TL + TCX] = inp["ctx"][b, qr * TCX:(qr + 1) * TCX]
    return np.ascontiguousarray(tok.T.reshape(DC, 128, T).transpose(1, 0, 2))


def host_c(inp, r):
    b = r // 4
    cc = np.stack([np.asarray(inp["c"][b], np.float32), np.asarray(inp["c_ctx"], np.float32)], 1)
    return np.ascontiguousarray(cc.reshape(DC, 128, 2).transpose(1, 0, 2))


def common_inputs(inp, r, layers):
    m = {}
    m["cT"] = host_c(inp, r)
    m.update(host_consts(r % 4))
    fg = np.asarray(inp["final_g"], np.float32).reshape(DC, 128).T
    m["final_g"] = np.ascontiguousarray(fg)
    return m


def run_stage_set(stages, fused, inp, state, wcache):
    b = Builder(list(stages), fused)
    b.cc_inc = 16
    nc = b.nc
    b.build()
    in_maps = []
    layers = sorted(set(int(s[1]) for s in stages))
    for r in range(8):
        m = common_inputs(inp, r, layers)
        for l in layers:
            m.update(wcache[l])
        for name in b.ins:
            if name in state[r]:
                m[name] = state[r][name]
        m = {k: v for k, v in m.items() if k in b.ins}
        missing = [k for k in b.ins if k not in m]
        assert not missing, missing
        in_maps.append(m)
    res = run_bass_kernel_spmd(nc, in_maps, core_ids=list(range(8)))
    return res.results


def kernel(**inp):
    inp = {k: np.asarray(v) for k, v in inp.items()}
    wcache = {l: host_weights(inp, l) for l in range(DEPTH)}
    state = [dict(xT=host_x(inp, r)) for r in range(8)]
    plan = [["A0", "B0", "A1", "B1"]] if FUSED else [["A0"], ["B0", "A1"], ["B1"]]
    for si, stages in enumerate(plan):
        results = run_stage_set(stages, FUSED, inp, state, wcache)
        new_state = [dict() for _ in range(8)]
        for r in range(8):
            out = results[r]
            if "xT_out" in out:
                new_state[r]["xT"] = out["xT_out"]
            for k in out:
                if k.startswith("qT_d") or k.startswith("sg_d"):
                    new_state[r][k] = out[k]
        for grp in ([0, 1, 2, 3], [4, 5, 6, 7]):
            for k in results[grp[0]]:
                for pre in ("kT_d", "v_d", "f_d"):
                    if k.startswith(pre):
                        g = np.concatenate([results[r][k] for r in grp], 0)
                        for r in grp:
                            new_state[r][k.replace("_d", "_g")] = g
        state = new_state
        last = results
    out = np.empty((2, SEQ, D), np.float32)
    for r in range(8):
        b_, qr = r // 4, r % 4
        oT = last[r]["outT"]
        out[b_, qr * TL:(qr + 1) * TL] = oT.transpose(2, 1, 0).reshape(TL, D)
    return out
```

```python
import math
import numpy as np
import ml_dtypes
import concourse.bass as bass
import concourse.mybir as mybir
from concourse.bass_utils import run_bass_kernel_spmd

F32 = mybir.dt.float32
BF16 = mybir.dt.bfloat16
AF = mybir.ActivationFunctionType
ALU = mybir.AluOpType
AX = mybir.AxisListType

D = 1024
DC = 8
DEPTH = 2
SEQ = 8192
CTX = 256
TL = 2048
TCX = 64
T = 2176
NKT = 17
TGS = [(0, 512), (512, 512), (1024, 512), (1536, 512), (2048, 128)]
DFF = 2816
NJ = 22
HBLK = [4, 4, 4, 4, 4, 2]
EPS = 1e-6
NH = 8
VW = 130
INW = 5632
SB_BASE = 16512
SB_END = 227328

COMPUTE = ("pe", "act", "dve", "pool")
SAME_ENGINE_SYNC = True
import os
FOU_DBG = int(os.environ.get("FOU_DBG", "9"))
FUSED = False


class Op:
    __slots__ = ("eng", "fn", "deps", "is_dma", "sem", "semval", "flag", "incval", "slot")

    def __init__(self, eng, fn, is_dma=False):
        self.eng = eng
        self.fn = fn
        self.deps = []
        self.is_dma = is_dma
        self.sem = None
        self.semval = 0
        self.flag = False
        self.incval = 0


class Prog:
    def __init__(self, nc):
        self.nc = nc
        self.ops = {e: [] for e in ("pe", "act", "dve", "pool", "sp")}
        self.lastw = {}
        self.readers = {}
        self.dma_sems = {}
        self.ctx = []

    def enter(self, cm):
        v = cm.__enter__()
        self.ctx.append(cm)
        return v

    def new_sem(self, name):
        return self.enter(self.nc.semaphore(name))

    def _track(self, op, reads, writes):
        deps = op.deps
        for k in reads:
            w = self.lastw.get(k)
            if w is not None:
                deps.append(w)
        for k in writes:
            w = self.lastw.get(k)
            if w is not None:
                deps.append(w)
            rs = self.readers.get(k)
            if rs:
                deps.extend(rs)
        for k in reads:
            self.readers.setdefault(k, []).append(op)
        for k in writes:
            self.lastw[k] = op
            self.readers[k] = []

    def op(self, eng, fn, reads=(), writes=()):
        o = Op(eng, fn)
        self._track(o, list(reads) + ["PHASE"], writes)
        self.ops[eng].append(o)
        return o

    def dma(self, eng, fn, slot, reads=(), writes=(), inc=16):
        o = Op(eng, fn, is_dma=True)
        self._track(o, list(reads) + ["PHASE"], writes)
        o.slot = slot
        o.deps = [d for d in o.deps if not (d.is_dma and d.slot == slot and d.eng == eng)]
        ent = self.dma_sems.get(slot)
        if ent is None:
            ent = [self.new_sem("d_%d" % len(self.dma_sems)), 0]
            self.dma_sems[slot] = ent
        ent[1] += inc
        o.sem = ent[0]
        o.semval = ent[1]
        o.incval = inc
        self.ops[eng].append(o)
        return o

    def barrier(self, fn):
        o = Op("pool", fn)
        self._track(o, [], ["PHASE"])
        self.ops["pool"].append(o)
        return o

    def finalize(self):
        for e, lst in self.ops.items():
            for o in lst:
                for d in o.deps:
                    if not d.is_dma:
                        if d.eng == o.eng and (d.eng == "pe" or not SAME_ENGINE_SYNC):
                            continue
                        d.flag = True
        self.esem = {}
        for e in COMPUTE:
            cnt = 0
            for o in self.ops[e]:
                if o.is_dma:
                    continue
                if o.flag:
                    cnt += 1
                    o.incval = cnt
            if cnt:
                self.esem[e] = self.new_sem("p_" + e)

    def emit_engine(self, e, engobj):
        waited = {}
        for o in self.ops[e]:
            need = {}
            for d in o.deps:
                if d.is_dma:
                    s, v = d.sem, d.semval
                else:
                    if d.eng == e and (e == "pe" or not SAME_ENGINE_SYNC):
                        continue
                    s, v = self.esem[d.eng], d.incval
                key = id(s)
                if v > need.get(key, (None, 0))[1]:
                    need[key] = (s, v)
            for key, (s, v) in need.items():
                if waited.get(key, 0) >= v:
                    continue
                engobj.wait_ge(s, v)
                waited[key] = v
            ins = o.fn()
            if o.is_dma:
                ins.then_inc(o.sem, o.incval)
            elif o.flag:
                ins.then_inc(self.esem[e], 1)
        sems = {}
        for o in self.ops[e]:
            if o.is_dma:
                sems[id(o.sem)] = (o.sem, max(o.semval, sems.get(id(o.sem), (None, 0))[1]))
        for key, (s, v) in sems.items():
            if waited.get(key, 0) < v:
                engobj.wait_ge(s, v)

    def run_blocks(self):
        nc = self.nc
        self.finalize()
        with nc.Block() as block:
            @block.tensor
            def _(eng):
                self.emit_engine("pe", eng)

            @block.scalar
            def _(eng):
                self.emit_engine("act", eng)

            @block.vector
            def _(eng):
                self.emit_engine("dve", eng)

            @block.gpsimd
            def _(eng):
                self.emit_engine("pool", eng)

            @block.sync
            def _(eng):
                self.emit_engine("sp", eng)

    def close(self):
        for cm in reversed(self.ctx):
            cm.__exit__(None, None, None)
        self.ctx = []


class Arena:
    def __init__(self, nc, start, end):
        self.nc, self.start, self.end, self.cur = nc, start, end, start
        self.n = 0

    def alloc(self, name, shape, dtype):
        esz = 4 if dtype == F32 else 2
        nbytes = int(np.prod(shape[1:])) * esz
        off = (self.cur + 63) // 64 * 64
        assert off + nbytes <= self.end, (name, off, nbytes, self.end)
        self.cur = off + nbytes
        self.n += 1
        return self.nc.alloc_sbuf_tensor_at(name, list(shape), dtype, offset=off)

    def sub(self):
        return Arena(self.nc, self.cur, self.end)


class Builder:
    def __init__(self, stages, fused, debug=False):
        self.stages = stages
        self.fused = fused
        self.debug = debug
        self.nc = nc = bass.Bass("TRN2", target_bir_lowering=False)
        self.P = Prog(nc)
        self.uid = 0
        self.ins = {}
        self.outs = {}

    def din(self, name, shape, dtype=F32):
        t = self.nc.dram_tensor(name, list(shape), dtype, kind="ExternalInput").ap()
        self.ins[name] = t
        return t

    def dout(self, name, shape, dtype=F32):
        t = self.nc.dram_tensor(name, list(shape), dtype, kind="ExternalOutput").ap()
        self.outs[name] = t
        return t

    def dint(self, name, shape, dtype=F32):
        return self.nc.dram_tensor(name, list(shape), dtype, kind="Internal").ap()

    def key(self, s):
        self.uid += 1
        return "%s#%d" % (s, self.uid)

    def mm(self, out, lhsT, rhs, start, stop, reads, writes, **kw):
        nc = self.nc
        self.P.op("pe", lambda: nc.tensor.matmul(out, lhsT=lhsT, rhs=rhs, start=start, stop=stop, **kw), reads, writes)

    def act(self, out, in_, func, reads, writes, **kw):
        nc = self.nc
        self.P.op("act", lambda: nc.scalar.activation(out=out, in_=in_, func=func, **kw), reads, writes)

    def tt(self, eng, out, in0, in1, op, reads, writes):
        nc = self.nc
        e = nc.vector if eng == "dve" else nc.gpsimd
        self.P.op(eng, lambda: e.tensor_tensor(out=out, in0=in0, in1=in1, op=op), reads, writes)

    def ts(self, eng, out, in0, s1, s2, op0, op1, reads, writes):
        nc = self.nc
        e = nc.vector if eng == "dve" else nc.gpsimd
        if op1 is None:
            self.P.op(eng, lambda: e.tensor_scalar(out=out, in0=in0, scalar1=s1, scalar2=None, op0=op0), reads, writes)
        else:
            self.P.op(eng, lambda: e.tensor_scalar(out=out, in0=in0, scalar1=s1, scalar2=s2, op0=op0, op1=op1), reads, writes)

    def stt(self, eng, out, in0, scalar, in1, op0, op1, reads, writes):
        nc = self.nc
        e = nc.vector if eng == "dve" else nc.gpsimd
        self.P.op(eng, lambda: e.scalar_tensor_tensor(out=out, in0=in0, scalar=scalar, in1=in1, op0=op0, op1=op1), reads, writes)

    def copy(self, eng, out, in_, reads, writes):
        nc = self.nc
        if eng == "act":
            self.P.op("act", lambda: nc.scalar.copy(out=out, in_=in_), reads, writes)
        else:
            e = nc.vector if eng == "dve" else nc.gpsimd
            self.P.op(eng, lambda: e.tensor_copy(out=out, in_=in_), reads, writes)

    def load(self, out, in_, slot, reads, writes, q="sp"):
        nc = self.nc
        if q == "sp":
            self.P.dma("sp", lambda: nc.sync.dma_start(out=out, in_=in_), slot, reads, writes)
        else:
            self.P.dma("pool", lambda: nc.gpsimd.dma_start(out=out, in_=in_), slot, reads, writes)

    def build(self):
        nc, P = self.nc, self.P
        st = self.stages
        first_l = int(st[0][1])
        self.xT_in = self.din("xT", [128, DC, T])
        self.cT = self.din("cT", [128, DC, 2])
        self.cossin = self.din("cossin", [128, 2, T])
        self.consts = self.din("consts", [128, 4, 128])
        self.dftch = self.din("dftch", [128, 2, 128], BF16)
        self.dftL = self.din("dftL", [8, 128, 64, 2, 256], BF16)
        self.dftC = self.din("dftC", [128, 4, 2, 256], BF16)
        layers = sorted(set(int(s[1]) for s in st))
        self.W = {}
        for l in layers:
            w = {}
            w["w_mod"] = self.din("w_mod%d" % l, [D, 9 * D])
            w["b_mod"] = self.din("b_mod%d" % l, [128, 72, 2])
            w["norm_g"] = self.din("norm_g%d" % l, [128, 3, DC])
            w["fw_in"] = self.din("fw_in%d" % l, [2, D, NJ * 256])
            w["fw_out"] = self.din("fw_out%d" % l, [2, DFF, D])
            w["w_in"] = self.din("w_in%d" % l, [D, INW])
            w["b_gate"] = self.din("b_gate%d" % l, [128, 16])
            w["lam"] = self.din("lam%d" % l, [128, 256])
            w["subln"] = self.din("subln%d" % l, [128, 128])
            w["w_ao"] = self.din("w_ao%d" % l, [D, D])
            w["w_fo"] = self.din("w_fo%d" % l, [512, D])
            w["w_o"] = self.din("w_o%d" % l, [D, D])
            self.W[l] = w
        self.final_g = self.din("final_g", [128, DC])

        self.X = {}
        for l in layers:
            e = {}
            hasA = ("A%d" % l) in st
            hasB = ("B%d" % l) in st
            def xt(kind, key, name, rows, cols):
                t32 = kind(name, [rows, cols // 2], F32)
                e.setdefault(key + "32", []).append(t32)
                e.setdefault(key, []).append(t32.bitcast(BF16))
            kd = self.dint if self.fused else self.dout
            kg = self.dint if self.fused else self.din
            if self.fused or hasA:
                for h in range(NH):
                    xt(kd, "kT_d", "kT_d%d_%d" % (l, h), 128, T)
                    xt(kd, "v_d", "v_d%d_%d" % (l, h), T, VW)
                for g in range(4):
                    xt(kd, "f_d", "f_d%d_%d" % (l, g), T, 128)
            if self.fused or hasB:
                for h in range(NH):
                    xt(kg, "kT_g", "kT_g%d_%d" % (l, h), 4 * 128, T)
                    xt(kg, "v_g", "v_g%d_%d" % (l, h), 4 * T, VW)
                for g in range(4):
                    xt(kg, "f_g", "f_g%d_%d" % (l, g), 4 * T, 128)
            if self.fused or (hasA and hasB):
                e["qT_d"] = self.dint("qT_d%d" % l, [NH * 128, T], BF16)
                e["sg_d"] = self.dint("sg_d%d" % l, [16 * 128, T], BF16)
            elif hasA:
                e["qT_d"] = self.dout("qT_d%d" % l, [NH * 128, T], BF16)
                e["sg_d"] = self.dout("sg_d%d" % l, [16 * 128, T], BF16)
            else:
                e["qT_d"] = self.din("qT_d%d" % l, [NH * 128, T], BF16)
                e["sg_d"] = self.din("sg_d%d" % l, [16 * 128, T], BF16)
            e["aT_d"] = (self.dout if self.debug else self.dint)("aT_d%d" % l, [NH * 128, T], BF16)
            self.X[l] = e
        last_stage = st[-1]
        if last_stage == "B1":
            self.out_ap = self.dout("outT", [128, DC, TL])
        else:
            self.out_ap = self.dout("xT_out", [128, DC, T])

        A = Arena(nc, SB_BASE, SB_END)
        self.XT = A.alloc("XT", [128, DC, T], F32)
        self.cst = A.alloc("cst", [128, 4, 128], F32)
        self.ones = self.cst[:, 0, :]
        self.RT = self.cst[:, 1, :]
        self.ident = self.cst[:, 2, :]
        self.epsb = self.cst[:, 3, :]
        self.bar = A.alloc("bar", [128, 8], F32)
        self.dch = A.alloc("dch", [128, 2, 128], BF16)
        self.mT = A.alloc("mT", [128, 72, 2], F32)
        self.bm = A.alloc("bm", [128, 72, 2], F32)
        self.ng = A.alloc("ng", [128, 3, DC], F32)
        self.gsc = A.alloc("gsc", [128, 3, DC, 2], F32)
        self.gth = A.alloc("gth", [128, 3, DC, 2], F32)
        self.silc = A.alloc("silc", [128, DC, 2], F32)
        self.bg = A.alloc("bg", [128, 16], F32)
        self.lamt = A.alloc("lamt", [128, 256], F32)
        self.lamv = A.alloc("lamv", [128, 8], F32)
        self.sgb = A.alloc("sgb", [128, 128], F32)
        self.fg = A.alloc("fg", [128, DC], F32)
        self.A0 = A.sub()
        self.ps = nc.alloc_psum_tensor("ps", [128, 4096], F32)

        self.load(self.XT[:], self.xT_in, "ld_x", [], ["XT"])
        self.load(self.cst[:], self.consts, "ld_c", [], ["cst"])
        self.load(self.dch[:], self.dftch, "ld_c2", [], ["dch"])
        self.load(self.fg[:], self.final_g, "ld_c3", [], ["fg"])

        self.barrier()
        for s in st:
            l = int(s[1])
            if s[0] == "A":
                self.phase_mod(l)
                self.barrier()
                self.ffn(l, 0)
                self.barrier()
                self.inproj(l)
                self.barrier()
                if self.fused:
                    self.exchange(l)
                    self.barrier()
            else:
                if first_l == l and st[0] == s:
                    self.phase_mod(l)
                    self.barrier()
                only = getattr(self, "only", ("att", "fou", "mrg", "ffn"))
                if "att" in only:
                    self.attention(l)
                    self.barrier()
                if "fou" in only:
                    self.fourier(l)
                    self.barrier()
                    if self.debug:
                        yd = self.dout("YT_dbg", [128, 4, T], BF16)
                        self.P.dma("sp", lambda: nc.sync.dma_start(out=yd, in_=self.YT[:]), "st_y", ["YT"], ["ydbg"])
                if "mrg" in only:
                    self.merge(l)
                    self.barrier()
                if "ffn" in only:
                    self.ffn(l, 1)
                    self.barrier()
        if last_stage == "B1":
            self.final_norm()
        else:
            nc_ = nc
            self.P.dma("sp", lambda: nc_.sync.dma_start(out=self.out_ap, in_=self.XT[:]), "st_x", ["XT"], ["out"])
        P.run_blocks()
        P.close()
        return nc

    def barrier(self):
        nc = self.nc
        self.P.barrier(lambda: nc.gpsimd.memset(self.bar[:, 0:1], 0.0))

    def bank(self, b, n=512, off=0):
        return self.ps[:, b * 512 + off: b * 512 + off + n]

    def phase_mod(self, l):
        nc, P = self.nc, self.P
        W = self.W[l]
        A = self.A0.sub()
        L = "m"
        wst = [A.alloc("wm%d_%d" % (l, i), [128, DC, 512], F32) for i in range(2)]
        cin = A.alloc("cin%d" % l, [128, DC, 2], F32)
        self.load(cin[:], self.cT, L + "c", [], [L + "cin"])
        self.load(self.bm[:], W["b_mod"], L + "b", ["bm"], ["bm"])
        self.load(self.ng[:], W["norm_g"], L + "g", ["ng"], ["ng"])
        self.load(self.bg[:], W["b_gate"], L + "bg", ["bg"], ["bg"])
        self.load(self.lamt[:], W["lam"], L + "lam", ["lamt"], ["lamt"])
        self.load(self.sgb[:], W["subln"], L + "sg", ["sgb"], ["sgb"])
        self.act(self.silc[:], cin[:], AF.Silu, [L + "cin"], ["silc"])
        wv = W["w_mod"].rearrange("(c p) n -> p c n", p=128)
        NS = 18
        for s in range(NS):
            buf = wst[s % 2]
            bk = "wm%d" % (s % 2)
            self.load(buf[:], wv[:, :, s * 512:(s + 1) * 512], (L, "w", s % 2), [], [bk])
            for jj in range(4):
                j = s * 4 + jj
                for c in range(DC):
                    self.mm(self.ps[:, j * 2:(j + 1) * 2], buf[:, c, jj * 128:(jj + 1) * 128], self.silc[:, c, :],
                            c == 0, c == DC - 1, [bk, "silc"], [("ps", 0)], skip_group_check=True)
        self.tt("dve", self.mT[:], self.ps[:, 0:144].rearrange("p (j s) -> p j s", s=2), self.bm[:], ALU.add,
                [("ps", 0), "bm"], ["mT"])
        for i in range(3):
            for s in range(2):
                sc = self.mT[:, (3 * i + 1) * 8:(3 * i + 2) * 8, s]
                gt = self.mT[:, (3 * i + 2) * 8:(3 * i + 3) * 8, s]
                self.stt("dve", self.gsc[:, i, :, s], sc, 1.0, self.ng[:, i, :], ALU.add, ALU.mult,
                         ["mT", "ng"], ["gsc"])
                self.ts("dve", self.gth[:, i, :, s], gt, 0.5 if i != 1 else 1.0, None, ALU.mult, None, ["mT"], ["gth"])
        lam_init = 0.8 - 0.6 * math.exp(-0.3 * l)
        tmp = A.alloc("lamtmp%d" % l, [128, 128], F32)
        self.tt("dve", tmp[:, 0:64], self.lamt[:, 0:64], self.lamt[:, 64:128], ALU.mult, ["lamt"], [L + "lt"])
        self.tt("dve", tmp[:, 64:128], self.lamt[:, 128:192], self.lamt[:, 192:256], ALU.mult, ["lamt"], [L + "lt"])
        P.op("dve", lambda: nc.vector.reduce_sum(out=self.lamv[:, 0:1], in_=tmp[:, 0:64], axis=AX.X), [L + "lt"], ["lamv"])
        P.op("dve", lambda: nc.vector.reduce_sum(out=self.lamv[:, 1:2], in_=tmp[:, 64:128], axis=AX.X), [L + "lt"], ["lamv"])
        self.act(self.lamv[:, 2:4], self.lamv[:, 0:2], AF.Exp, ["lamv"], ["lamv"])
        self.tt("dve", self.lamv[:, 4:5], self.lamv[:, 3:4], self.lamv[:, 2:3], ALU.subtract, ["lamv"], ["lamv"])
        self.ts("dve", self.lamv[:, 5:6], self.lamv[:, 4:5], -lam_init, None, ALU.add, None, ["lamv"], ["lamv"])
        self.ts("dve", self.sgb[:], self.sgb[:], 1.0 - lam_init, None, ALU.mult, None, ["sgb"], ["sgb"])

    def norm_mod(self, l, i, uT, ukey, A):
        L = "nm"
        sq = [A.alloc(L + "sq%d" % k, [128, 512], F32) for k in range(2)]
        rs = [A.alloc(L + "rs%d" % k, [128, 512], F32) for k in range(2)]
        tm = [A.alloc(L + "tm%d" % k, [128, 512], F32) for k in range(2)]
        n_ = 0
        for g, (t0, n) in enumerate(TGS):
            pb = 6 + (g % 2)
            for c in range(DC):
                k = n_ % 2
                n_ += 1
                self.act(sq[k][:, :n], self.XT[:, c, t0:t0 + n], AF.Square, ["XT"], [L + "sq%d" % k])
                self.mm(self.bank(pb, n), self.ones, sq[k][:, :n], c == 0, c == DC - 1, [L + "sq%d" % k, "cst"], [("ps", pb)])
            r = rs[g % 2]
            rk = L + "rs%d" % (g % 2)
            self.act(r[:, :n], self.bank(pb, n), AF.Sqrt, [("ps", pb)], [rk], scale=1.0 / D, bias=self.epsb[:, 0:1])
            self.P.op("dve", lambda r=r, n=n: self.nc.vector.reciprocal(out=r[:, :n], in_=r[:, :n]), [rk], [rk])
            s = 1 if g == 4 else 0
            for c in range(DC):
                k = c % 2
                self.tt("dve", tm[k][:, :n], self.XT[:, c, t0:t0 + n], r[:, :n], ALU.mult, ["XT", rk], [L + "tm%d" % k])
                if g < 4:
                    self.ts("pool", uT[:, c, t0:t0 + n], tm[k][:, :n], self.gsc[:, i, c, 0:1], self.mT[:, (3 * i) * 8 + c, 0:1],
                            ALU.mult, ALU.add, [L + "tm%d" % k, "gsc", "mT"], [(ukey, c, g)])
                else:
                    self.ts("pool", uT[:, c, t0:t0 + n], tm[k][:, :n], self.gsc[:, i, c, 1:2], self.mT[:, (3 * i) * 8 + c, 1:2],
                            ALU.mult, ALU.add, [L + "tm%d" % k, "gsc", "mT"], [(ukey, c, g)])

    def ffn(self, l, f):
        nc, P = self.nc, self.P
        i = 0 if f == 0 else 2
        W = self.W[l]
        A = self.A0.sub()
        L = "f"
        uT = A.alloc(L + "u", [128, DC, T], BF16)
        hg = A.alloc(L + "hg", [128, 4, T], BF16)
        NWI = 3
        wi = [A.alloc(L + "wi%d" % k, [128, DC, 512], BF16) for k in range(NWI)]
        wo = [A.alloc(L + "wo%d" % k, [128, 4, D], BF16) for k in range(2)]
        sa = [A.alloc(L + "sa%d" % k, [128, 512], F32) for k in range(2)]
        self.norm_mod(l, i, uT, L + "u", A)
        w_in = W["fw_in"][f].rearrange("(c p) n -> p c n", p=128)
        w_out = W["fw_out"][f].rearrange("(j p) n -> p j n", p=128)
        j0 = 0
        nsl = 0
        cnt = 0
        jstart = [sum(HBLK[:hb]) for hb in range(len(HBLK))]

        def issue_wo(hb):
            if hb >= len(HBLK):
                return
            wob_, HB_, j0_ = wo[hb % 2], HBLK[hb], jstart[hb]
            P.dma("pool", lambda: nc.gpsimd.dma_start(out=wob_[:, 0:HB_, :], in_=w_out[:, j0_:j0_ + HB_, :]),
                  (L, "wo", hb % 2), [], [L + "wo%d" % (hb % 2)])

        def issue_wi(si):
            if si >= NJ // 2:
                return
            wb_ = wi[si % NWI]
            P.dma("pool", lambda: nc.gpsimd.dma_start(out=wb_[:], in_=w_in[:, :, si * 512:(si + 1) * 512]),
                  (L, "wi", si % NWI), [], [L + "wi%d" % (si % NWI)])
        issue_wo(0)
        issue_wi(0)
        issue_wi(1)
        for hb, HB in enumerate(HBLK):
            wob = wo[hb % 2]
            wok = L + "wo%d" % (hb % 2)
            issue_wo(hb + 1)
            for sl in range(HB // 2):
                jj = j0 + sl * 2
                wb = wi[nsl % NWI]
                wk = L + "wi%d" % (nsl % NWI)
                issue_wi(nsl + 2)
                nsl += 1
                for q in range(2):
                    jl = sl * 2 + q
                    for g, (t0, n) in enumerate(TGS):
                        pa, pb = (0, 1) if cnt % 2 == 0 else (2, 3)
                        k = cnt % 2
                        cnt += 1
                        for c in range(DC):
                            self.mm(self.bank(pa, n), wb[:, c, q * 256:q * 256 + 128], uT[:, c, t0:t0 + n], c == 0, c == DC - 1,
                                    [wk, (L + "u", c, g)], [("ps", pa)])
                        for c in range(DC):
                            self.mm(self.bank(pb, n), wb[:, c, q * 256 + 128:q * 256 + 256], uT[:, c, t0:t0 + n], c == 0, c == DC - 1,
                                    [wk, (L + "u", c, g)], [("ps", pb)])
                        self.act(sa[k][:, :n], self.bank(pa, n), AF.Silu, [("ps", pa)], [L + "sa%d" % k])
                        self.tt("dve", hg[:, jl, t0:t0 + n], sa[k][:, :n], self.bank(pb, n), ALU.mult,
                                [L + "sa%d" % k, ("ps", pb)], [(L + "hg", jl, g)])
            oc_cnt = 0
            for oc in range(DC):
                for g, (t0, n) in enumerate(TGS):
                    pb = 4 + (oc_cnt % 2)
                    oc_cnt += 1
                    for jl in range(HB):
                        self.mm(self.bank(pb, n), wob[:, jl, oc * 128:(oc + 1) * 128], hg[:, jl, t0:t0 + n], jl == 0, jl == HB - 1,
                                [wok, (L + "hg", jl, g)], [("ps", pb)])
                    s = 1 if g == 4 else 0
                    self.stt("dve", self.XT[:, oc, t0:t0 + n], self.bank(pb, n), self.gth[:, i, oc, s:s + 1], self.XT[:, oc, t0:t0 + n],
                             ALU.mult, ALU.add, [("ps", pb), "gth", "XT"], ["XT"])
            j0 += HB

    def inproj(self, l):
        nc, P = self.nc, self.P
        W = self.W[l]
        E = self.X[l]
        A = self.A0.sub()
        L = "p"
        uT = A.alloc(L + "u", [128, DC, T], BF16)
        NWI = 3
        wi = [A.alloc(L + "wi%d" % k, [128, DC, 512], BF16) for k in range(NWI)]
        cs = A.alloc(L + "cs", [128, 2, T], F32)
        qf = [A.alloc(L + "qf%d" % k, [128, 512], F32) for k in range(2)]
        t1 = [A.alloc(L + "t1%d" % k, [128, 512], F32) for k in range(2)]
        t2 = [A.alloc(L + "t2%d" % k, [128, 512], F32) for k in range(2)]
        qo = [A.alloc(L + "qo%d" % k, [128, T], BF16) for k in range(2)]
        vs = [A.alloc(L + "vs%d" % k, [128, 4, VW], BF16) for k in range(2)]
        fs = [A.alloc(L + "fs%d" % k, [128, 512], BF16) for k in range(2)]
        self.load(cs[:], self.cossin, L + "cs", [], [L + "cs"])
        self.norm_mod(l, 1, uT, L + "u", A)
        wv = W["w_in"].rearrange("(c p) n -> p c n", p=128)
        for k in range(2):
            P.op("pool", lambda k=k: nc.gpsimd.memset(vs[k][:], 0.0), [], [L + "vs%d" % k])
            P.op("pool", lambda k=k: nc.gpsimd.memset(vs[k][:, :, 128:129], 1.0), [L + "vs%d" % k], [L + "vs%d" % k])
        cnt = 0
        ocnt = 0
        vcnt = 0
        def issue_wi(si):
            if si >= 11:
                return
            wb_ = wi[si % NWI]
            P.dma("pool", lambda: nc.gpsimd.dma_start(out=wb_[:], in_=wv[:, :, si * 512:(si + 1) * 512]),
                  (L, "wi", si % NWI), [], [L + "wi%d" % (si % NWI)])
        issue_wi(0)
        issue_wi(1)
        for sl in range(11):
            wb = wi[sl % NWI]
            wk = L + "wi%d" % (sl % NWI)
            issue_wi(sl + 2)
            if sl < 4:
                for hc in range(4):
                    h = (sl % 2) * 4 + hc
                    ob = qo[ocnt % 2]
                    ok = L + "qo%d" % (ocnt % 2)
                    ocnt += 1
                    for g, (t0, n) in enumerate(TGS):
                        k = cnt % 2
                        pa, pb = (0, 1) if k == 0 else (2, 3)
                        cnt += 1
                        for c in range(DC):
                            self.mm(self.bank(pa, n), wb[:, c, hc * 128:(hc + 1) * 128], uT[:, c, t0:t0 + n], c == 0, c == DC - 1,
                                    [wk, (L + "u", c, g)], [("ps", pa)])
                        self.copy("act", qf[k][:, :n], self.bank(pa, n), [("ps", pa)], [L + "qf%d" % k])
                        self.mm(self.bank(pb, n), self.RT, qf[k][:, :n], True, True, [L + "qf%d" % k, "cst"], [("ps", pb)])
                        self.tt("dve", t1[k][:, :n], qf[k][:, :n], cs[:, 0, t0:t0 + n], ALU.mult, [L + "qf%d" % k, L + "cs"], [L + "t1%d" % k])
                        self.tt("dve", t2[k][:, :n], self.bank(pb, n), cs[:, 1, t0:t0 + n], ALU.mult, [("ps", pb), L + "cs"], [L + "t2%d" % k])
                        self.tt("pool", ob[:, t0:t0 + n], t1[k][:, :n], t2[k][:, :n], ALU.add, [L + "t1%d" % k, L + "t2%d" % k], [ok])
                    dst = E["qT_d"][h * 128:(h + 1) * 128, :] if sl < 2 else E["kT_d"][h]
                    P.dma("sp", lambda ob=ob, dst=dst: nc.sync.dma_start(out=dst, in_=ob[:]),
                          (L, "qo", (ocnt - 1) % 2), [ok], [(L, "qk_d", sl < 2)])
            elif sl < 7:
                for tt_ in range(NKT):
                    k = cnt % 2
                    pa = 0 if k == 0 else 2
                    cnt += 1
                    g = min(tt_ // 4, 4)
                    for c in range(DC):
                        self.mm(self.bank(pa), uT[:, c, tt_ * 128:(tt_ + 1) * 128], wb[:, c, :], c == 0, c == DC - 1,
                                [wk, (L + "u", c, g)], [("ps", pa)])
                    if sl < 6:
                        vb = vs[vcnt % 2]
                        vk = L + "vs%d" % (vcnt % 2)
                        self.copy("act", vb[:, :, 0:128], self.bank(pa).rearrange("p (h e) -> p h e", e=128), [("ps", pa)], [vk])
                        if tt_ == NKT - 1:
                            P.op("pool", lambda vb=vb: nc.gpsimd.memset(vb[64:128, :, :], 0.0), [vk], [vk])
                        hh = (sl - 4) * 4
                        for h4 in range(4):
                            P.dma("sp", lambda vb=vb, tt_=tt_, hh=hh, h4=h4: nc.sync.dma_start(
                                out=E["v_d"][hh + h4][tt_ * 128:(tt_ + 1) * 128, :], in_=vb[:, h4, :]),
                                (L, "vs", vcnt % 2), [vk], [(L, "v_d")])
                        if tt_ == NKT - 1:
                            P.op("pool", lambda vb=vb: nc.gpsimd.memset(vb[64:128, :, 128:129], 1.0), [vk], [vk])
                        vcnt += 1
                    else:
                        fb = fs[vcnt % 2]
                        fk = L + "fs%d" % (vcnt % 2)
                        self.copy("act", fb[:], self.bank(pa), [("ps", pa)], [fk])
                        for g4 in range(4):
                            P.dma("sp", lambda fb=fb, tt_=tt_, g4=g4: nc.sync.dma_start(out=E["f_d"][g4][tt_ * 128:(tt_ + 1) * 128, :], in_=fb[:, g4 * 128:(g4 + 1) * 128]),
                                  (L, "fs", vcnt % 2), [fk], [(L, "f_d")])
                        vcnt += 1
            else:
                for hc in range(4):
                    gc = (sl - 7) * 4 + hc
                    ob = qo[ocnt % 2]
                    ok = L + "qo%d" % (ocnt % 2)
                    ocnt += 1
                    for g, (t0, n) in enumerate(TGS):
                        k = cnt % 2
                        pa = 0 if k == 0 else 2
                        cnt += 1
                        for c in range(DC):
                            self.mm(self.bank(pa, n), wb[:, c, hc * 128:(hc + 1) * 128], uT[:, c, t0:t0 + n], c == 0, c == DC - 1,
                                    [wk, (L + "u", c, g)], [("ps", pa)])
                        self.act(ob[:, t0:t0 + n], self.bank(pa, n), AF.Sigmoid, [("ps", pa), "bg"], [ok], bias=self.bg[:, gc:gc + 1])
                    P.dma("sp", lambda ob=ob, gc=gc: nc.sync.dma_start(out=E["sg_d"][gc * 128:(gc + 1) * 128, :], in_=ob[:]),
                          (L, "qo", (ocnt - 1) % 2), [ok], [(L, "sg_d")])

    def exchange(self, l):
        nc, P = self.nc, self.P
        E = self.X[l]
        L = "x"
        rg = [[0, 1, 2, 3], [4, 5, 6, 7]]
        for nm, src, dst, rk, cnt in (("k", "kT_d", "kT_g", ("p", "qk_d", False), NH),
                                      ("v", "v_d", "v_g", ("p", "v_d"), NH),
                                      ("f", "f_d", "f_g", ("p", "f_d"), 4)):
            for i in range(cnt):
                P.dma("pool", lambda src=src, dst=dst, i=i: nc.gpsimd.collective_compute(
                    "AllGather", ALU.bypass, replica_groups=rg, ins=[E[src + "32"][i]], outs=[E[dst + "32"][i]]),
                    (L, "cc"), [rk], [(L, dst)], inc=1)

    def attention(self, l):
        nc, P = self.nc, self.P
        E = self.X[l]
        A = self.A0.sub()
        L = "a"
        NKTG = 4 * NKT
        KT = [A.alloc(L + "KT%d" % k, [128, 4 * T], BF16) for k in range(2)]
        V = [A.alloc(L + "V%d" % k, [128, NKTG, VW], BF16) for k in range(2)]
        Q = [A.alloc(L + "Q%d" % k, [128, T], BF16) for k in range(2)]
        NPT = 3
        pT = [A.alloc(L + "pT%d" % k, [128, 2, 512], BF16) for k in range(NPT)]
        on = [A.alloc(L + "on%d" % k, [128, 128], F32) for k in range(4)]
        o1 = [A.alloc(L + "o1%d" % k, [128, 128], F32) for k in range(4)]
        sqj = A.alloc(L + "sqj", [128, 128], F32)
        sm = [A.alloc(L + "sm%d" % k, [128, 8], F32) for k in range(4)]
        aS = [A.alloc(L + "aS%d" % k, [128, 512], BF16) for k in range(2)]
        gx = "x"
        kdep = [(gx, "kT_g")] if self.fused else []
        vdep = [(gx, "v_g")] if self.fused else []
        qdep = [("p", "qk_d", True)]
        scale = 64 ** -0.5
        groups = TGS if l == 0 else TGS[:4]
        groups = groups[:int(os.environ.get("ATT_G", "9"))]
        NHX = min(NH, int(os.environ.get("ATT_H", "9")))
        ptc = 0
        fc = 0
        ac = 0
        def issue_head(h):
            if h >= NH:
                return
            kb_, vb_, qb_ = KT[h % 2], V[h % 2], Q[h % 2]
            kk_, vk_, qk_ = L + "KT%d" % (h % 2), L + "V%d" % (h % 2), L + "Q%d" % (h % 2)
            self.load(qb_[:], E["qT_d"][h * 128:(h + 1) * 128, :], (L, "q", h % 2), qdep, [qk_])
            vg3 = E["v_g"][h].rearrange("(kt p) n -> p kt n", p=128)
            for r in range(4):
                self.load(kb_[:, r * T:(r + 1) * T], E["kT_g"][h][r * 128:(r + 1) * 128, :], (L, "k", h % 2), kdep, [kk_])
            for part in range(4):
                self.load(vb_[:, part * NKT:(part + 1) * NKT, :], vg3[:, part * NKT:(part + 1) * NKT, :],
                          (L, "v", h % 2), vdep, [vk_])
        issue_head(0)
        for h in range(NHX):
            kb, vb, qb = KT[h % 2], V[h % 2], Q[h % 2]
            kk, vk, qk = L + "KT%d" % (h % 2), L + "V%d" % (h % 2), L + "Q%d" % (h % 2)
            issue_head(h + 1)
            for g, (t0, n) in enumerate(groups):
                nsub = n // 128
                kts = list(range(NKTG)) if g < 4 else [NKT - 1 + NKT * r for r in range(4)]
                def acc(c, s):
                    idx = c * 4 + s
                    return self.ps[:, (4 + idx // 3) * 512 + (idx % 3) * 129:(4 + idx // 3) * 512 + (idx % 3) * 129 + 129]
                active = [cc * 4 + ss for cc in range(2) for ss in range(nsub)]

                def scores(ki):
                    nonlocal ptc
                    kt = kts[ki]
                    sb = (ki % 2) * 2
                    pt = pT[ptc % NPT]
                    pk = L + "pT%d" % (ptc % NPT)
                    ptc += 1
                    for c in range(2):
                        self.mm(self.bank(sb + c, n), kb[c * 64:(c + 1) * 64, kt * 128:(kt + 1) * 128], qb[c * 64:(c + 1) * 64, t0:t0 + n],
                                True, True, [kk, qk], [("ps", sb + c)])
                    self.act(pt[:, :, :n], self.ps[:, sb * 512:(sb + 2) * 512].rearrange("p (c n) -> p c n", c=2)[:, :, :n], AF.Exp,
                             [("ps", sb), ("ps", sb + 1)], [pk], scale=scale)
                    return pt, pk

                pend = scores(0)
                for ki, kt in enumerate(kts):
                    nxt = scores(ki + 1) if ki + 1 < len(kts) else None
                    pt, pk = pend
                    for c in range(2):
                        for s in range(nsub):
                            idx = c * 4 + s
                            first = (ki == 0) and idx == min(a for a in active if a // 3 == idx // 3)
                            self.mm(acc(c, s), pt[:, c, s * 128:(s + 1) * 128], vb[:, kt, 0:129], first, ki == len(kts) - 1,
                                    [pk, vk], [("ps", 4 + idx // 3)], skip_group_check=True)
                    pend = nxt
                ab = aS[ac % 2]
                ak = L + "aS%d" % (ac % 2)
                ac += 1
                for s in range(nsub):
                    k = s
                    smk = L + "sm%d" % k
                    b0 = ("ps", 4 + s // 3)
                    b1 = ("ps", 4 + (4 + s) // 3)
                    a0, a1 = acc(0, s), acc(1, s)
                    P.op("dve", lambda k=k, a0=a0: nc.vector.reciprocal(out=sm[k][:, 0:1], in_=a0[:, 128:129]), [b0], [smk])
                    P.op("dve", lambda k=k, a1=a1: nc.vector.reciprocal(out=sm[k][:, 1:2], in_=a1[:, 128:129]), [b1], [smk])
                    self.tt("dve", sm[k][:, 2:3], sm[k][:, 1:2], self.lamv[:, 5:6], ALU.mult, [smk, "lamv"], [smk])
                    self.ts("dve", o1[k][:], a1[:, 0:128], sm[k][:, 2:3], None, ALU.mult, None, [b1, smk], [L + "o1%d" % k])
                    self.stt("dve", on[k][:], a0[:, 0:128], sm[k][:, 0:1], o1[k][:], ALU.mult, ALU.add, [b0, smk, L + "o1%d" % k], [L + "on%d" % k])
                for s in range(nsub):
                    k = s
                    smk = L + "sm%d" % k
                    self.tt("pool", o1[k][:], on[k][:], on[k][:], ALU.mult, [L + "on%d" % k], [L + "o1%d" % k])
                    P.op("dve", lambda k=k: nc.vector.reduce_sum(out=sm[k][:, 3:4], in_=o1[k][:], axis=AX.X), [L + "o1%d" % k], [smk])
                    self.act(sm[k][:, 4:5], sm[k][:, 3:4], AF.Sqrt, [smk], [smk], scale=1.0 / 128, bias=self.epsb[:, 0:1])
                    P.op("dve", lambda k=k: nc.vector.reciprocal(out=sm[k][:, 5:6], in_=sm[k][:, 4:5]), [smk], [smk])
                    self.stt("dve", on[k][:], on[k][:], sm[k][:, 5:6], self.sgb[:], ALU.mult, ALU.mult, [L + "on%d" % k, smk, "sgb"], [L + "on%d" % k])
                    kk2 = s % 2
                    P.op("pe", lambda k=k, kk2=kk2: nc.tensor.transpose(self.ps[:, 7 * 512 + (kk2 * 128):7 * 512 + (kk2 + 1) * 128], on[k][:], self.ident),
                         [L + "on%d" % k, "cst"], [("ps7", kk2)])
                    self.copy("dve", ab[:, s * 128:(s + 1) * 128], self.ps[:, 7 * 512 + (kk2 * 128):7 * 512 + (kk2 + 1) * 128], [("ps7", kk2)], [ak])
                P.dma("sp", lambda ab=ab, h=h, t0=t0, n=n: nc.sync.dma_start(out=E["aT_d"][h * 128:(h + 1) * 128, t0:t0 + n], in_=ab[:, :n]),
                      (L, "aS", (ac - 1) % 2), [ak], [(L, "aT_d")])

    def fourier(self, l):
        nc, P = self.nc, self.P
        E = self.X[l]
        L = "r"
        yoff = SB_END - 4 * T * 2
        yoff = yoff // 64 * 64
        self.YT = nc.alloc_sbuf_tensor_at(L + "YT", [128, 4, T], BF16, offset=yoff)
        A = Arena(nc, self.A0.cur, yoff)
        Xs = A.alloc(L + "X", [128, 64, 512], BF16)
        Xc = A.alloc(L + "Xc", [128, 4, 512], BF16)
        NDB = 3
        NTB = 8
        db = [A.alloc(L + "db%d" % k, [128, NTB, 2, 256], BF16) for k in range(NDB)]
        dc = A.alloc(L + "dc", [128, 4, 2, 256], BF16)
        pq = [A.alloc(L + "pq%d" % k, [128, 8, 256], BF16) for k in range(2)]
        fdep = [("x", "f_g")] if self.fused else []
        for r in range(4):
            for g4 in range(4):
                fg = E["f_g"][g4]
                self.load(Xs[:, r * 16:(r + 1) * 16, g4 * 128:(g4 + 1) * 128], fg[r * T:r * T + TL, :].rearrange("(i p) n -> p i n", p=128),
                          (L, "X"), fdep, [L + "X"])
                self.load(Xc[:, r, g4 * 128:(g4 + 1) * 128], fg[r * T + TL:r * T + TL + 128, :], (L, "Xc"), fdep, [L + "Xc"])
        self.load(dc[:], self.dftC, L + "dc", [], [L + "dc"])
        dcnt = 0
        ycnt = 0
        kgs = list(range(8)) + ([8] if l == 0 else [])
        NB = 64 // NTB

        def issue_db(i):
            if i >= 8 * NB:
                return
            d_ = db[i % NDB]
            self.load(d_[:], self.dftL[i // NB, :, (i % NB) * NTB:(i % NB + 1) * NTB, :, :], (L, "db", i % NDB), [], [L + "db%d" % (i % NDB)])
        issue_db(0)
        issue_db(1)
        for kg in kgs:
            def accp(gi, csi, n=256):
                idx = gi * 2 + csi
                return self.ps[:, (idx // 2) * 512 + (idx % 2) * 256:(idx // 2) * 512 + (idx % 2) * 256 + n]
            if kg < 8:
                for nb in range(64 // NTB):
                    d = db[dcnt % NDB]
                    dk = L + "db%d" % (dcnt % NDB)
                    issue_db(dcnt + 2)
                    dcnt += 1
                    for ni in range(NTB):
                        nt = nb * NTB + ni
                        for gi in range(4):
                            for csi in range(2):
                                idx = gi * 2 + csi
                                first = (nt == 0) and (idx % 2 == 0)
                                if FOU_DBG >= 2:
                                    self.mm(accp(gi, csi), Xs[:, nt, gi * 128:(gi + 1) * 128], d[:, ni, csi, :], first, nt == 63,
                                            [L + "X", dk], [("ps", idx // 2)], skip_group_check=True)
                                else:
                                    self.copy("dve", self.bar[:, 1:2], d[:, ni, csi, 0:1], [dk, L + "X"], ["bar1"])
                ncol = 256
                t0 = kg * 256
            else:
                ncol = 256
                t0 = TL
                for r in range(4 if FOU_DBG != 5 else 0):
                    for gi in range(4):
                        for csi in range(2):
                            idx = gi * 2 + csi
                            first = (r == 0) and (idx % 2 == 0)
                            self.mm(accp(gi, csi), Xc[:, r, gi * 128:(gi + 1) * 128], dc[:, r, csi, :], first, r == 3,
                                    [L + "Xc", L + "dc"], [("ps", idx // 2)], skip_group_check=True)
            if FOU_DBG < 3 or (FOU_DBG == 5 and kg == 8):
                continue
            pb = pq[ycnt % 2]
            pk = L + "pq%d" % (ycnt % 2)
            ycnt += 1
            for b in range(4):
                src = self.ps[:, b * 512:(b + 1) * 512].rearrange("p (i n) -> p i n", i=2)[:, :, :ncol]
                self.copy("act" if b % 2 == 0 else "dve", pb[:, 2 * b:2 * b + 2, :ncol], src, [("ps", b)], [(pk, b)])
            if FOU_DBG < 4:
                continue
            for gi in range(4):
                ybank = 4 + (gi % 2)
                for csi in range(2):
                    self.mm(self.bank(ybank, ncol), self.dch[:, csi, :], pb[:, gi * 2 + csi, :ncol], csi == 0, csi == 1,
                            [(pk, gi), "dch"], [("ps", ybank)])
                nw = ncol if kg < 8 else 128
                self.copy("dve", self.YT[:, gi, t0:t0 + nw], self.bank(ybank, nw), [("ps", ybank)], ["YT"])

    def merge(self, l):
        nc, P = self.nc, self.P
        W = self.W[l]
        E = self.X[l]
        L = "g"
        yoff = (SB_END - 4 * T * 2) // 64 * 64
        A = Arena(nc, self.A0.cur, yoff)
        wao = A.alloc(L + "wao", [128, 8, D], BF16)
        wfo = A.alloc(L + "wfo", [128, 4, D], BF16)
        wo = A.alloc(L + "wo", [128, 8, D], BF16)
        sg = [A.alloc(L + "sg%d" % k, [128, 16, 512], BF16) for k in range(2)]
        aT = [A.alloc(L + "aT%d" % k, [128, 8, 512], BF16) for k in range(2)]
        mg = A.alloc(L + "mg", [128, 8, 512], BF16)
        ta = [A.alloc(L + "ta%d" % k, [128, 512], F32) for k in range(2)]
        tb = [A.alloc(L + "tb%d" % k, [128, 512], F32) for k in range(2)]
        P.dma("pool", lambda: nc.gpsimd.dma_start(out=wao[:], in_=W["w_ao"].rearrange("(c p) n -> p c n", p=128)), L + "w1", [], [L + "wao"])
        P.dma("pool", lambda: nc.gpsimd.dma_start(out=wfo[:], in_=W["w_fo"].rearrange("(c p) n -> p c n", p=128)), L + "w2", [], [L + "wfo"])
        P.dma("pool", lambda: nc.gpsimd.dma_start(out=wo[:], in_=W["w_o"].rearrange("(c p) n -> p c n", p=128)), L + "w3", [], [L + "wo"])
        sg3 = E["sg_d"].rearrange("(c p) t -> p c t", p=128)
        a3 = E["aT_d"].rearrange("(c p) t -> p c t", p=128)
        sgdep = [("p", "sg_d")]
        adep = [("a", "aT_d")]
        groups = TGS if l == 0 else TGS[:4]
        cnt = 0
        for g, (t0, n) in enumerate(groups):
            sb = sg[g % 2]
            sk = L + "sg%d" % (g % 2)
            ab = aT[g % 2]
            ak = L + "aT%d" % (g % 2)
            self.load(sb[:, :, :n], sg3[:, :, t0:t0 + n], (L, "sg", g % 2), sgdep, [sk])
            self.load(ab[:, :, :n], a3[:, :, t0:t0 + n], (L, "aT", g % 2), adep, [ak])
            for oc in range(DC):
                k = cnt % 2
                cnt += 1
                pa, pf = (0, 1) if k == 0 else (2, 3)
                for h in range(NH):
                    self.mm(self.bank(pa, n), wao[:, h, oc * 128:(oc + 1) * 128], ab[:, h, :n], h == 0, h == NH - 1, [L + "wao", ak], [("ps", pa)])
                for gi in range(4):
                    self.mm(self.bank(pf, n), wfo[:, gi, oc * 128:(oc + 1) * 128], self.YT[:, gi, t0:t0 + n], gi == 0, gi == 3, [L + "wfo", "YT"], [("ps", pf)])
                self.tt("dve", ta[k][:, :n], self.bank(pa, n), sb[:, oc, :n], ALU.mult, [("ps", pa), sk], [L + "ta%d" % k])
                self.tt("dve", tb[k][:, :n], self.bank(pf, n), sb[:, 8 + oc, :n], ALU.mult, [("ps", pf), sk], [L + "tb%d" % k])
                self.tt("pool", mg[:, oc, :n], ta[k][:, :n], tb[k][:, :n], ALU.add, [L + "ta%d" % k, L + "tb%d" % k], [(L + "mg", oc)])
            s = 1 if g == 4 else 0
            for oc in range(DC):
                pb = 4 + (oc % 2)
                for c in range(DC):
                    self.mm(self.bank(pb, n), wo[:, c, oc * 128:(oc + 1) * 128], mg[:, c, :n], c == 0, c == DC - 1, [L + "wo", (L + "mg", c)], [("ps", pb)])
                self.stt("dve", self.XT[:, oc, t0:t0 + n], self.bank(pb, n), self.gth[:, 1, oc, s:s + 1], self.XT[:, oc, t0:t0 + n],
                         ALU.mult, ALU.add, [("ps", pb), "gth", "XT"], ["XT"])

    def final_norm(self):
        nc, P = self.nc, self.P
        A = self.A0.sub()
        L = "fin"
        sq = [A.alloc(L + "sq%d" % k, [128, 512], F32) for k in range(2)]
        rs = [A.alloc(L + "rs%d" % k, [128, 512], F32) for k in range(2)]
        ot = [A.alloc(L + "ot%d" % k, [128, DC, 512], F32) for k in range(2)]
        n_ = 0
        for g, (t0, n) in enumerate(TGS[:4]):
            pb = 6 + (g % 2)
            for c in range(DC):
                k = n_ % 2
                n_ += 1
                self.act(sq[k][:, :n], self.XT[:, c, t0:t0 + n], AF.Square, ["XT"], [L + "sq%d" % k])
                self.mm(self.bank(pb, n), self.ones, sq[k][:, :n], c == 0, c == DC - 1, [L + "sq%d" % k, "cst"], [("ps", pb)])
            r = rs[g % 2]
            rk = L + "rs%d" % (g % 2)
            self.act(r[:, :n], self.bank(pb, n), AF.Sqrt, [("ps", pb)], [rk], scale=1.0 / D, bias=self.epsb[:, 0:1])
            P.op("dve", lambda r=r, n=n: nc.vector.reciprocal(out=r[:, :n], in_=r[:, :n]), [rk], [rk])
            o = ot[g % 2]
            okk = L + "ot%d" % (g % 2)
            for c in range(DC):
                self.stt("dve", o[:, c, :n], self.XT[:, c, t0:t0 + n], self.fg[:, c:c + 1], r[:, :n], ALU.mult, ALU.mult, ["XT", "fg", rk], [okk])
            P.dma("sp", lambda o=o, t0=t0, n=n: nc.sync.dma_start(out=self.out_ap[:, :, t0:t0 + n], in_=o[:, :, :n]), (L, "ot", g % 2), [okk], ["out"])


_CONST_CACHE = {}


def _bf(a):
    return np.ascontiguousarray(a.astype(ml_dtypes.bfloat16))


def host_consts(qr):
    if qr in _CONST_CACHE:
        return _CONST_CACHE[qr]
    GRID_W = 64
    AXIS = 32
    inv = 1.0 / (10000.0 ** (np.arange(0, AXIS, 2, dtype=np.float32) / AXIS))
    pos = qr * TL + np.arange(TL)
    row = (pos // GRID_W).astype(np.float32)
    col = (pos % GRID_W).astype(np.float32)
    cos = np.ones((128, T), np.float32)
    sin = np.zeros((128, T), np.float32)
    for cbase in (0, 64):
        for ax, p_ in ((0, row), (1, col)):
            ang = p_[None, :] * inv[:, None]
            ang = np.concatenate([ang, ang], 0)
            sgn = np.concatenate([-np.ones(16), np.ones(16)]).astype(np.float32)[:, None]
            sl = slice(cbase + ax * 32, cbase + ax * 32 + 32)
            cos[sl, :TL] = np.cos(ang)
            sin[sl, :TL] = np.sin(ang) * sgn
    cossin = np.stack([cos, sin], 1).astype(np.float32)
    RT = np.zeros((128, 128), np.float32)
    for m in range(128):
        blk = m // 32 * 32
        o = m - blk
        k = blk + (o + 16) % 32
        RT[k, m] = 1.0
    consts = np.zeros((128, 4, 128), np.float32)
    consts[:, 0, :] = 1.0
    consts[:, 1, :] = RT
    consts[:, 2, :] = np.eye(128, dtype=np.float32)
    consts[:, 3, :] = EPS
    cc = np.arange(128)
    angc = 2 * np.pi * ((cc[:, None] * cc[None, :]) % 128) / 128.0
    dftch = np.stack([np.cos(angc), -np.sin(angc)], 1) / np.sqrt(128.0)
    tabc = np.cos(2 * np.pi * np.arange(SEQ) / SEQ) / np.sqrt(SEQ)
    tabs = np.sin(2 * np.pi * np.arange(SEQ) / SEQ) / np.sqrt(SEQ)
    n = (np.arange(64)[None, :] * 128 + np.arange(128)[:, None]).astype(np.int64)
    dftL = np.empty((8, 128, 64, 2, 256), ml_dtypes.bfloat16)
    for kg in range(8):
        k = qr * TL + kg * 256 + np.arange(256, dtype=np.int64)
        idx = (n[:, :, None] * k[None, None, :]) % SEQ
        dftL[kg, :, :, 0, :] = tabc[idx].astype(ml_dtypes.bfloat16)
        dftL[kg, :, :, 1, :] = tabs[idx].astype(ml_dtypes.bfloat16)
    dftC = np.zeros((128, 4, 2, 256), np.float32)
    nn = np.arange(4)[None, :] * 64 + np.arange(64)[:, None]
    kk = qr * 64 + np.arange(64)
    ang = 2 * np.pi * ((nn[:, :, None] * kk[None, None, :]) % CTX) / CTX
    dftC[:64, :, 0, :64] = np.cos(ang) / np.sqrt(CTX)
    dftC[:64, :, 1, :64] = np.sin(ang) / np.sqrt(CTX)
    out = dict(cossin=cossin, consts=consts, dftch=_bf(dftch), dftL=dftL, dftC=_bf(dftC))
    _CONST_CACHE[qr] = out
    return out


def host_weights(inp, l):
    w = {}
    f32 = np.float32
    w["w_mod%d" % l] = np.ascontiguousarray(inp["w_mod"][l], f32)
    bm = np.asarray(inp["b_mod"][l], f32).reshape(72, 128).T
    w["b_mod%d" % l] = np.ascontiguousarray(np.repeat(bm[:, :, None], 2, 2))
    w["norm_g%d" % l] = np.ascontiguousarray(np.asarray(inp["norm_g"][l], f32).reshape(3, DC, 128).transpose(2, 0, 1))
    fwi = np.asarray(inp["ffn_w_in"][l], f32)
    a = fwi[:, :, :DFF].reshape(2, D, NJ, 1, 128)
    b = fwi[:, :, DFF:].reshape(2, D, NJ, 1, 128)
    w["fw_in%d" % l] = np.ascontiguousarray(np.concatenate([a, b], 3).reshape(2, D, NJ * 256))
    w["fw_out%d" % l] = np.ascontiguousarray(inp["ffn_w_out"][l], f32)
    w["w_in%d" % l] = np.ascontiguousarray(inp["w_in"][l], f32)
    bg = np.asarray(inp["b_gate"][l], f32).reshape(2, DC, 128)
    w["b_gate%d" % l] = np.ascontiguousarray(bg.reshape(16, 128).T)
    w["lam%d" % l] = np.ascontiguousarray(np.broadcast_to(np.asarray(inp["lam_params"][l], f32).reshape(1, 256), (128, 256)))
    w["subln%d" % l] = np.ascontiguousarray(np.broadcast_to(np.asarray(inp["subln_g"][l], f32).reshape(1, 128), (128, 128)))
    w["w_ao%d" % l] = np.ascontiguousarray(inp["w_attn_out"][l], f32)
    w["w_fo%d" % l] = np.ascontiguousarray(inp["w_four_out"][l], f32)
    w["w_o%d" % l] = np.ascontiguousarray(inp["w_o"][l], f32)
    return w


def host_x(inp, r):
    b, qr = r // 4, r % 4
    tok = np.zeros((T, D), np.float32)
    tok[:TL] = inp["x"][b, qr * TL:(qr + 1) * TL]
    tok[TL:TL + TCX] = inp["ctx"][b, qr * TCX:(qr + 1) * TCX]
    return np.ascontiguousarray(tok.T.reshape(DC, 128, T).transpose(1, 0, 2))


def host_c(inp, r):
    b = r // 4
    cc = np.stack([np.asarray(inp["c"][b], np.float32), np.asarray(inp["c_ctx"], np.float32)], 1)
    return np.ascontiguousarray(cc.reshape(DC, 128, 2).transpose(1, 0, 2))


def common_inputs(inp, r, layers):
    m = {}
    m["cT"] = host_c(inp, r)
    m.update(host_consts(r % 4))
    fg = np.asarray(inp["final_g"], np.float32).reshape(DC, 128).T
    m["final_g"] = np.ascontiguousarray(fg)
    return m


def run_stage_set(stages, fused, inp, state, wcache):
    b = Builder(list(stages), fused)
    b.cc_inc = 16
    nc = b.nc
    b.build()
    in_maps = []
    layers = sorted(set(int(s[1]) for s in stages))
    for r in range(8):
        m = common_inputs(inp, r, layers)
        for l in layers:
            m.update(wcache[l])
        for name in b.ins:
            if name in state[r]:
                m[name] = state[r][name]
        m = {k: v for k, v in m.items() if k in b.ins}
        missing = [k for k in b.ins if k not in m]
        assert not missing, missing
        in_maps.append(m)
    res = run_bass_kernel_spmd(nc, in_maps, core_ids=list(range(8)))
    return res.results


def kernel(**inp):
    inp = {k: np.asarray(v) for k, v in inp.items()}
    wcache = {l: host_weights(inp, l) for l in range(DEPTH)}
    state = [dict(xT=host_x(inp, r)) for r in range(8)]
    plan = [["A0", "B0", "A1", "B1"]] if FUSED else [["A0"], ["B0", "A1"], ["B1"]]
    for si, stages in enumerate(plan):
        results = run_stage_set(stages, FUSED, inp, state, wcache)
        new_state = [dict() for _ in range(8)]
        for r in range(8):
            out = results[r]
            if "xT_out" in out:
                new_state[r]["xT"] = out["xT_out"]
            for k in out:
                if k.startswith("qT_d") or k.startswith("sg_d"):
                    new_state[r][k] = out[k]
        for grp in ([0, 1, 2, 3], [4, 5, 6, 7]):
            for k in results[grp[0]]:
                for pre in ("kT_d", "v_d", "f_d"):
                    if k.startswith(pre):
                        g = np.concatenate([results[r][k] for r in grp], 0)
                        for r in grp:
                            new_state[r][k.replace("_d", "_g")] = g
        state = new_state
        last = results
    out = np.empty((2, SEQ, D), np.float32)
    for r in range(8):
        b_, qr = r // 4, r % 4
        oT = last[r]["outT"]
        out[b_, qr * TL:(qr + 1) * TL] = oT.transpose(2, 1, 0).reshape(TL, D)
    return out
```
